# Optimizing a Trainium2 kernel written in Bass

```python
import math
import jax
import jax.numpy as jnp
from jax import lax
import numpy as np

D_MODEL = 2048
BATCH = 4
SEQ = 2048
DEPTH = 2
DEC_BATCH = 128
DEC_SEQ = 8
PAST_LEN = 2048
PAGE_SIZE = 128

HEAD_DIM = 128
EPS = 1e-6
NEG_INF = -1e30
Q_BLOCK = 128

CONV_WIDTH = 31
CONV_BUF = CONV_WIDTH - 1
D_CONV = D_MODEL // 2
POOL_WINDOWS = (2, 4, 8, 16)
N_POOL_GROUPS = len(POOL_WINDOWS)
D_POOL = D_MODEL // 2
POOL_GROUP = D_POOL // N_POOL_GROUPS
POOL_BUF = max(POOL_WINDOWS) - 1
AB_IN = 2 * D_CONV + D_POOL
AB_OUT = D_CONV + D_POOL

NSA_HEADS = (D_MODEL // 2) // HEAD_DIM
NSA_KV_HEADS = 2
NSA_GROUP = NSA_HEADS // NSA_KV_HEADS
NSA_BLOCK = 64
NSA_TOPN = 16
NSA_WINDOW = 512
NSA_BRANCHES = 3
NSA_KV_W = 2 * NSA_KV_HEADS * HEAD_DIM
DSA_HEADS = (D_MODEL // 2) // HEAD_DIM
DSA_KV_HEADS = 2
DSA_GROUP = DSA_HEADS // DSA_KV_HEADS
DSA_KV_W = 2 * DSA_KV_HEADS * HEAD_DIM
IDX_HEADS = 8
IDX_DIM = 64
DSA_TOPK_MAX = 256
CD_SPLITS = (NSA_HEADS * HEAD_DIM, NSA_KV_W, NSA_KV_W, NSA_KV_W, NSA_HEADS * NSA_BRANCHES,
             DSA_HEADS * HEAD_DIM, DSA_KV_W, IDX_HEADS * IDX_DIM, IDX_DIM, IDX_HEADS)
CD_IN = sum(CD_SPLITS)
CD_OUT = (NSA_HEADS + DSA_HEADS) * HEAD_DIM

N_ATTN_HEADS = NSA_HEADS + DSA_HEADS
N_BUCKETS = 32
MAX_DISTANCE = 128
D_FF = 4 * D_MODEL
N_EVEN = (DEPTH + 1) // 2
N_ODD = DEPTH // 2

kernel_name = "hybrid_conv_pool_nsa_dsa_decode_step"


def rmsnorm(x, g):
    xf = x.astype(jnp.float32)
    y = xf * lax.rsqrt(jnp.mean(xf * xf, -1, keepdims=True) + EPS)
    return (y * g.astype(jnp.float32)).astype(x.dtype)


def layernorm(x, g, b):
    xf = x.astype(jnp.float32)
    mu = jnp.mean(xf, -1, keepdims=True)
    xc = xf - mu
    y = xc * lax.rsqrt(jnp.mean(xc * xc, -1, keepdims=True) + EPS)
    return (y * g.astype(jnp.float32) + b.astype(jnp.float32)).astype(x.dtype)


def split_cols(a, sizes):
    return jnp.split(a, np.cumsum(sizes)[:-1].tolist(), axis=-1)


def rel_bucket(dist):
    n = jnp.maximum(dist, 0)
    exact = N_BUCKETS // 2
    nf = jnp.maximum(n, 1).astype(jnp.float32)
    big = exact + (jnp.log(nf / exact) / math.log(MAX_DISTANCE / exact) * (N_BUCKETS - exact)).astype(jnp.int32)
    return jnp.where(n < exact, n, jnp.minimum(big, N_BUCKETS - 1))


def masked_softmax(s, mask):
    s = jnp.where(mask, s, NEG_INF)
    m = jnp.max(s, -1, keepdims=True)
    p = jnp.exp(s - m) * mask.astype(jnp.float32)
    return p / jnp.maximum(jnp.sum(p, -1, keepdims=True), 1e-30)


def map_query_blocks(fn, qb, *arrays):
    B, T = arrays[0].shape[:2]
    nb = T // qb
    xs = tuple(jnp.swapaxes(a.reshape((B, nb, qb) + a.shape[2:]), 0, 1) for a in arrays)
    out = lax.map(lambda args: fn(*args), xs)
    return jax.tree_util.tree_map(lambda o: jnp.swapaxes(o, 0, 1).reshape((B, T) + o.shape[3:]), out)


def conv_module(a, gate, buf, w_dw, b_dw, ln_g, ln_b):
    u = a * jax.nn.sigmoid(gate)
    ext = jnp.concatenate([buf, u], 1)
    c = lax.conv_general_dilated(ext, w_dw[:, None, :], (1,), 'VALID',
                                 dimension_numbers=('NWC', 'WIO', 'NWC'),
                                 feature_group_count=D_CONV) + b_dw
    return jax.nn.silu(layernorm(c, ln_g, ln_b)), ext[:, -CONV_BUF:]


def pool_mixer(v, buf, pos, w_grp, scale):
    ext = jnp.concatenate([buf, v], 1)
    ext_f = ext.astype(jnp.float32)
    csum = jnp.concatenate([jnp.zeros_like(ext_f[:, :1]), jnp.cumsum(ext_f, 1)], 1)
    T = v.shape[1]
    end = csum[:, POOL_BUF + 1:]
    outs = []
    for g, w in enumerate(POOL_WINDOWS):
        sl = slice(g * POOL_GROUP, (g + 1) * POOL_GROUP)
        start = csum[:, POOL_BUF + 1 - w:POOL_BUF + 1 - w + T, sl]
        cnt = jnp.minimum(pos + 1, w).astype(jnp.float32)[None, :, None]
        outs.append((end[..., sl] - start) / cnt - ext_f[:, POOL_BUF:, sl])
    d = jnp.stack(outs, 2).astype(v.dtype)
    y = jnp.einsum('btgc,gcd->btgd', d, w_grp).reshape(v.shape) * scale
    return y, ext[:, -POOL_BUF:]


def ab_mixer(h, pos, conv_buf, pool_buf, w_in, conv_w, conv_b, ln_g, ln_b, pool_w, pool_scale, w_out):
    a, gate, v = split_cols(h @ w_in, (D_CONV, D_CONV, D_POOL))
    yc, new_conv = conv_module(a, gate, conv_buf, conv_w, conv_b, ln_g, ln_b)
    yp, new_pool = pool_mixer(v, pool_buf, pos, pool_w, pool_scale)
    return jnp.concatenate([yc, yp], -1) @ w_out, new_conv, new_pool


def cd_mixer(h, pos, pos0, past_cmp, past_sel, past_dsa, past_idx, win_buf, win_keep, qb,
             w_in, w_cmp, rel_bias, w_out):
    B, T, _ = h.shape
    dt = h.dtype
    G, R, DH = NSA_KV_HEADS, NSA_GROUP, HEAD_DIM
    scale = HEAD_DIM ** -0.5
    q_n, kv_c, kv_s, kv_w, gates, q_d, kv_d, q_i, k_i, w_i = split_cols(h @ w_in, CD_SPLITS)
    q_n = q_n.reshape(B, T, G, R, DH)
    kv_c = kv_c.reshape(B, T, 2, G, DH)
    kv_s = kv_s.reshape(B, T, 2, G, DH)
    kv_w = kv_w.reshape(B, T, 2, G, DH)
    gates = jax.nn.sigmoid(gates.astype(jnp.float32)).astype(dt).reshape(B, T, G, R, NSA_BRANCHES)
    q_d = q_d.reshape(B, T, DSA_KV_HEADS, DSA_GROUP, DH)
    kv_d = kv_d.reshape(B, T, 2, DSA_KV_HEADS, DH)
    q_i = q_i.reshape(B, T, IDX_HEADS, IDX_DIM)
    w_i = w_i * (IDX_HEADS ** -0.5)

    ctx_c = jnp.concatenate([past_cmp, kv_c], 1)
    ctx_s = jnp.concatenate([past_sel, kv_s], 1)
    ctx_d = jnp.concatenate([past_dsa, kv_d], 1)
    ctx_i = jnp.concatenate([past_idx, k_i], 1)
    L = ctx_c.shape[1]
    qpos = jnp.broadcast_to(pos[None], (B, T))
    bias_n = rel_bias[:, :NSA_HEADS].reshape(N_BUCKETS, G, R)
    bias_n_g = jnp.transpose(bias_n, (1, 0, 2))
    bias_d = rel_bias[:, NSA_HEADS:].reshape(N_BUCKETS, DSA_KV_HEADS, DSA_GROUP)

    n_cmp = L // NSA_BLOCK
    blocks = ctx_c[:, :n_cmp * NSA_BLOCK].reshape(B, n_cmp, NSA_BLOCK, 2, G, DH)
    comp = jnp.einsum('bnjcgd,cjg->bncgd', blocks, w_cmp)
    ck, cv = comp[:, :, 0], comp[:, :, 1]
    blk_end = (jnp.arange(n_cmp, dtype=jnp.int32) + 1) * NSA_BLOCK - 1
    dist_c = pos[:, None] - blk_end[None]
    bias_c = jnp.transpose(bias_n[rel_bucket(dist_c)], (0, 2, 3, 1))[None]
    s_c = jnp.einsum('btgrd,bngd->btgrn', q_n, ck).astype(jnp.float32) * scale + bias_c
    p_c = masked_softmax(s_c, (dist_c >= 0)[None, :, None, None, :])
    o_c = jnp.einsum('btgrn,bngd->btgrd', p_c.astype(dt), cv)

    n_sel = -(-L // NSA_BLOCK)
    imp = jnp.pad(jnp.sum(p_c, 3), ((0, 0), (0, 0), (0, 0), (0, n_sel - n_cmp)))
    blk = jnp.arange(n_sel, dtype=jnp.int32)[None]
    cur = (pos // NSA_BLOCK)[:, None]
    imp = jnp.where((blk == cur)[None, :, None, :], 2.0,
                    jnp.where((blk > cur)[None, :, None, :], -1.0, imp))
    n_top = min(NSA_TOPN, n_sel)
    _, sel_idx = lax.top_k(imp, n_top)
    sel_blocks = jnp.pad(ctx_s, ((0, 0), (0, n_sel * NSA_BLOCK - L), (0, 0), (0, 0), (0, 0)))
    sel_blocks = jnp.transpose(sel_blocks.reshape(B, n_sel, NSA_BLOCK, 2, G, DH), (0, 4, 1, 2, 3, 5))

    wb = win_buf.shape[1]
    win_ext = jnp.concatenate([jnp.zeros((B, NSA_WINDOW - wb, 2, G, DH), dt), win_buf, kv_w], 1)
    b_ix = jnp.arange(B)[:, None, None, None]
    g_ix = jnp.arange(G)[None, None, :, None]

    def nsa_block(q_b, idx_b, qp_b):
        nq = q_b.shape[1]
        gath = sel_blocks[b_ix, g_ix, idx_b]
        K = n_top * NSA_BLOCK
        gath = gath.reshape(B, nq, G, K, 2, DH)
        kpos = (idx_b[..., None] * NSA_BLOCK + jnp.arange(NSA_BLOCK, dtype=jnp.int32)).reshape(B, nq, G, K)
        dist = qp_b[:, :, None, None] - kpos
        bias = jnp.swapaxes(bias_n_g[g_ix, rel_bucket(dist)], -1, -2)
        s = jnp.einsum('btgrd,btgkd->btgrk', q_b, gath[..., 0, :]).astype(jnp.float32) * scale + bias
        p = masked_softmax(s, (dist >= 0)[:, :, :, None, :])
        o_s = jnp.einsum('btgrk,btgkd->btgrd', p.astype(dt), gath[..., 1, :])
        start = qp_b[0, 0] - pos0
        kw = lax.dynamic_slice_in_dim(win_ext, start, NSA_WINDOW + nq, axis=1)
        kpos_w = pos0 - NSA_WINDOW + start + jnp.arange(NSA_WINDOW + nq, dtype=jnp.int32)
        dist_w = qp_b[0][:, None] - kpos_w[None]
        bias_w = jnp.transpose(bias_n[rel_bucket(dist_w)], (0, 2, 3, 1))[None]
        s_w = jnp.einsum('btgrd,bkgd->btgrk', q_b, kw[:, :, 0]).astype(jnp.float32) * scale + bias_w
        mask_w = ((dist_w >= 0) & (dist_w < NSA_WINDOW) & (kpos_w[None] >= 0))[None, :, None, None, :]
        p_w = masked_softmax(s_w, mask_w)
        o_w = jnp.einsum('btgrk,bkgd->btgrd', p_w.astype(dt), kw[:, :, 1])
        return o_s, o_w

    o_s, o_w = map_query_blocks(nsa_block, qb, q_n, sel_idx, qpos)
    o_nsa = gates[..., 0, None] * o_c + gates[..., 1, None] * o_s + gates[..., 2, None] * o_w
    o_nsa = o_nsa.reshape(B, T, NSA_HEADS * DH)

    k_sel = min(DSA_TOPK_MAX, L // 4)
    kpos_all = jnp.arange(L, dtype=jnp.int32)
    bt_ix = jnp.arange(B)[:, None, None]

    def dsa_block(qd_b, qi_b, wi_b, qp_b):
        sc = jnp.einsum('bthi,bsi->bths', qi_b, ctx_i).astype(jnp.float32) * (IDX_DIM ** -0.5)
        score = jnp.einsum('bths,bth->bts', jax.nn.relu(sc), wi_b.astype(jnp.float32))
        score = jnp.where(kpos_all[None, None, :] <= qp_b[:, :, None], score, NEG_INF)
        _, idx = lax.top_k(score, k_sel)
        gath = ctx_d[bt_ix, idx]
        dist = qp_b[:, :, None] - idx
        bias = jnp.transpose(bias_d[rel_bucket(dist)], (0, 1, 3, 4, 2))
        s = jnp.einsum('btgrd,btkgd->btgrk', qd_b, gath[:, :, :, 0]).astype(jnp.float32) * scale + bias
        p = masked_softmax(s, (dist >= 0)[:, :, None, None, :])
        return jnp.einsum('btgrk,btkgd->btgrd', p.astype(dt), gath[:, :, :, 1])

    o_d = map_query_blocks(dsa_block, qb, q_d, q_i, w_i, qpos).reshape(B, T, DSA_HEADS * DH)
    y = jnp.concatenate([o_nsa, o_d], -1) @ w_out
    return y, kv_c, kv_s, win_ext[:, -win_keep:], kv_d, k_i


def run_trunk(x, pos0, conv_bufs, pool_bufs, past_cmp, past_sel, past_dsa, past_idx, win_bufs,
              win_keep, qb, weights):
    (norm_mix, norm_ffn, norm_final, ab_w_in, ab_conv_w, ab_conv_b, ab_ln_g, ab_ln_b, ab_pool_w,
     ab_pool_scale, ab_w_out, cd_w_in, cd_w_cmp, cd_w_out, rel_bias, ffn_w1, ffn_w2) = weights
    T = x.shape[1]
    pos = pos0 + jnp.arange(T, dtype=jnp.int32)
    n_conv, n_pool, n_cmp, n_sel, n_win, n_dsa, n_idx = [], [], [], [], [], [], []
    for i in range(DEPTH):
        j = i // 2
        h = rmsnorm(x, norm_mix[i])
        if i % 2 == 0:
            y, c, p = ab_mixer(h, pos, conv_bufs[j], pool_bufs[j], ab_w_in[j], ab_conv_w[j], ab_conv_b[j],
                               ab_ln_g[j], ab_ln_b[j], ab_pool_w[j], ab_pool_scale[j], ab_w_out[j])
            n_conv.append(c)
            n_pool.append(p)
        else:
            y, kc, ks, kw, kd, ki = cd_mixer(h, pos, pos0, past_cmp[j], past_sel[j], past_dsa[j], past_idx[j],
                                             win_bufs[j], win_keep, qb, cd_w_in[j], cd_w_cmp[j], rel_bias,
                                             cd_w_out[j])
            n_cmp.append(kc)
            n_sel.append(ks)
            n_win.append(kw)
            n_dsa.append(kd)
            n_idx.append(ki)
        x = x + y
        hf = rmsnorm(x, norm_ffn[i])
        x = x + jnp.square(jax.nn.relu(hf @ ffn_w1[i])) @ ffn_w2[i]
    return (rmsnorm(x, norm_final), jnp.stack(n_conv), jnp.stack(n_pool), jnp.stack(n_cmp), jnp.stack(n_sel),
            jnp.stack(n_win), jnp.stack(n_dsa), jnp.stack(n_idx))


def setup_inputs(seed: int = 0) -> dict:
    key = jax.random.key(seed)
    ks = iter(jax.random.split(key, 40))
    f32 = jnp.float32

    def nrm(shape, s=1.0):
        return jax.random.normal(next(ks), shape, f32) * s

    n_pages = PAST_LEN // PAGE_SIZE
    n_used = DEC_BATCH * n_pages
    n_pool = n_used + (n_used + 3) // 4
    win_len = min(NSA_WINDOW, PAST_LEN)
    page_table = jax.random.permutation(next(ks), n_pool)[:n_used].reshape(DEC_BATCH, n_pages).astype(jnp.int32)
    return {
        "x_prompt": nrm((BATCH, SEQ, D_MODEL)),
        "x_sample": nrm((DEC_BATCH, DEC_SEQ, D_MODEL)),
        "state_conv": nrm((N_EVEN, DEC_BATCH, CONV_BUF, D_CONV), 0.5),
        "state_pool": nrm((N_EVEN, DEC_BATCH, POOL_BUF, D_POOL)),
        "cache_nsa_cmp": nrm((N_ODD, n_pool, PAGE_SIZE, 2, NSA_KV_HEADS, HEAD_DIM)),
        "cache_nsa_sel": nrm((N_ODD, n_pool, PAGE_SIZE, 2, NSA_KV_HEADS, HEAD_DIM)),
        "cache_nsa_win": nrm((N_ODD, DEC_BATCH, win_len, 2, NSA_KV_HEADS, HEAD_DIM)),
        "cache_dsa_kv": nrm((N_ODD, n_pool, PAGE_SIZE, 2, DSA_KV_HEADS, HEAD_DIM)),
        "cache_dsa_idx": nrm((N_ODD, n_pool, PAGE_SIZE, IDX_DIM)),
        "page_table": page_table,
        "norm_mix": 1.0 + nrm((DEPTH, D_MODEL), 0.05),
        "norm_ffn": 1.0 + nrm((DEPTH, D_MODEL), 0.05),
        "norm_final": 1.0 + nrm((D_MODEL,), 0.05),
        "ab_w_in": nrm((N_EVEN, D_MODEL, AB_IN), D_MODEL ** -0.5),
        "ab_conv_w": nrm((N_EVEN, CONV_WIDTH, D_CONV), CONV_WIDTH ** -0.5),
        "ab_conv_b": nrm((N_EVEN, D_CONV), 0.02),
        "ab_ln_g": 1.0 + nrm((N_EVEN, D_CONV), 0.05),
        "ab_ln_b": nrm((N_EVEN, D_CONV), 0.02),
        "ab_pool_w": nrm((N_EVEN, N_POOL_GROUPS, POOL_GROUP, POOL_GROUP), POOL_GROUP ** -0.5),
        "ab_pool_scale": 1.0 + nrm((N_EVEN, D_POOL), 0.05),
        "ab_w_out": nrm((N_EVEN, AB_OUT, D_MODEL), AB_OUT ** -0.5),
        "cd_w_in": nrm((N_ODD, D_MODEL, CD_IN), D_MODEL ** -0.5),
        "cd_w_cmp": (1.0 + nrm((N_ODD, 2, NSA_BLOCK, NSA_KV_HEADS), 0.1)) / NSA_BLOCK,
        "cd_w_out": nrm((N_ODD, CD_OUT, D_MODEL), CD_OUT ** -0.5),
        "rel_bias": nrm((N_BUCKETS, N_ATTN_HEADS), 0.5),
        "ffn_w1": nrm((DEPTH, D_MODEL, D_FF), D_MODEL ** -0.5),
        "ffn_w2": nrm((DEPTH, D_FF, D_MODEL), D_FF ** -0.5),
    }


def reference(x_prompt, x_sample, state_conv, state_pool, cache_nsa_cmp, cache_nsa_sel, cache_nsa_win,
              cache_dsa_kv, cache_dsa_idx, page_table, norm_mix, norm_ffn, norm_final, ab_w_in, ab_conv_w,
              ab_conv_b, ab_ln_g, ab_ln_b, ab_pool_w, ab_pool_scale, ab_w_out, cd_w_in, cd_w_cmp, cd_w_out,
              rel_bias, ffn_w1, ffn_w2):
    weights = (norm_mix, norm_ffn, norm_final, ab_w_in, ab_conv_w, ab_conv_b, ab_ln_g, ab_ln_b, ab_pool_w,
               ab_pool_scale, ab_w_out, cd_w_in, cd_w_cmp, cd_w_out, rel_bias, ffn_w1, ffn_w2)
    bp, tp = x_prompt.shape[:2]
    dtp = x_prompt.dtype
    kv_empty = jnp.zeros((N_ODD, bp, 0, 2, NSA_KV_HEADS, HEAD_DIM), dtp)
    dkv_empty = jnp.zeros((N_ODD, bp, 0, 2, DSA_KV_HEADS, HEAD_DIM), dtp)
    idx_empty = jnp.zeros((N_ODD, bp, 0, IDX_DIM), dtp)
    (y_prompt, conv_p, pool_p, cmp_p, sel_p, win_p, dsa_p, idx_p) = run_trunk(
        x_prompt, 0,
        jnp.zeros((N_EVEN, bp, CONV_BUF, D_CONV), dtp), jnp.zeros((N_EVEN, bp, POOL_BUF, D_POOL), dtp),
        kv_empty, kv_empty, dkv_empty, idx_empty, kv_empty,
        min(NSA_WINDOW, tp), min(Q_BLOCK, tp), weights)
    db = x_sample.shape[0]
    past_len = page_table.shape[1] * cache_nsa_cmp.shape[2]

    def gather_pages(pool):
        g = pool[:, page_table]
        return g.reshape((g.shape[0], db, past_len) + g.shape[4:])

    (y_sample, conv_s, pool_s, cmp_s, sel_s, win_s, dsa_s, idx_s) = run_trunk(
        x_sample, past_len, state_conv, state_pool,
        gather_pages(cache_nsa_cmp), gather_pages(cache_nsa_sel), gather_pages(cache_dsa_kv),
        gather_pages(cache_dsa_idx), cache_nsa_win,
        cache_nsa_win.shape[2], 1, weights)
    return (y_prompt, y_sample, conv_p, conv_s, pool_p, pool_s, cmp_p, cmp_s, sel_p, sel_s,
            win_p, win_s, dsa_p, dsa_s, idx_p, idx_s)
```

```python
import os
import numpy as np
from contextlib import ExitStack
import concourse.bass as bass
import concourse.mybir as mybir
from concourse.bass_utils import run_bass_kernel_spmd

F32 = mybir.dt.float32
BF16 = mybir.dt.bfloat16
I32 = mybir.dt.int32
AF = mybir.ActivationFunctionType
ALU = mybir.AluOpType
AX = mybir.AxisListType

D = 2048
DC = 1024
DFF = 8192
EPS = 1e-6
NCORES = 8
SAMPLE_ATTN = os.environ.get("KSAMPLE", "1") == "1"
POOL_WINDOWS = (2, 4, 8, 16)
CD_IN = 4704
C_KVC, C_KVS, C_KVW, C_KVD, C_KI = 1024, 1536, 2048, 3608, 4632


class Sched:
    ENG = ("pe", "act", "dve", "pool", "sp")

    def __init__(self, nc, n_dma_sems=24):
        self.nc = nc
        self.ops = {e: [] for e in self.ENG}
        self.cnt = {e: 0 for e in self.ENG}
        self.dma_cnt = {}
        self.dma_rr_q = {}
        self.n_dma_sems = n_dma_sems
        self.dma_rr = 0
        self.last_w = {}
        self.readers = {}
        self.seen = {e: {} for e in self.ENG}
        self.extra = {}

    def alias(self, new_names, old_names):
        merged = {}
        for o in old_names:
            t = self.last_w.get(o)
            if t is not None and merged.get(t[0], 0) < t[1]:
                merged[t[0]] = t[1]
            for k, v in self.readers.get(o, {}).items():
                if merged.get(k, 0) < v:
                    merged[k] = v
            for k, v in self.extra.get(o, {}).items():
                if merged.get(k, 0) < v:
                    merged[k] = v
        for n in new_names:
            d = self.extra.setdefault(n, {})
            for k, v in merged.items():
                if d.get(k, 0) < v:
                    d[k] = v

    def _tok_wait(self, eng, key, val, waits):
        if key == eng and eng == "pe":
            return
        prev = self.seen[eng].get(key, 0)
        if prev >= val:
            return
        self.seen[eng][key] = val
        waits.append((key, val))

    def op(self, eng, fn, reads=(), writes=(), dma=False):
        waits = []
        for b in list(reads) + list(writes):
            ex = self.extra.get(b)
            if ex:
                for k, v in ex.items():
                    self._tok_wait(eng, k, v, waits)
        for b in reads:
            t = self.last_w.get(b)
            if t is not None:
                self._tok_wait(eng, t[0], t[1], waits)
        for b in writes:
            t = self.last_w.get(b)
            if t is not None:
                self._tok_wait(eng, t[0], t[1], waits)
            for k, v in self.readers.get(b, {}).items():
                self._tok_wait(eng, k, v, waits)
        if dma:
            nsem = 12 if eng == "pool" else 28
            rrk = self.dma_rr_q.get(eng, 0)
            self.dma_rr_q[eng] = rrk + 1
            self.dma_rr += 1
            key = ("dma", eng, rrk % nsem)
            if self.dma_cnt.get(key, 0) > 0:
                self._tok_wait(eng, key, self.dma_cnt[key], waits)
            self.dma_cnt[key] = self.dma_cnt.get(key, 0) + 16
            tok = (key, self.dma_cnt[key])
            inc = (key, 16)
        else:
            self.cnt[eng] += 1
            tok = (eng, self.cnt[eng])
            inc = (eng, 1)
        self.ops[eng].append((waits, fn, inc))
        for b in reads:
            r = self.readers.setdefault(b, {})
            if r.get(tok[0], 0) < tok[1]:
                r[tok[0]] = tok[1]
        for b in writes:
            self.last_w[b] = tok
            self.readers[b] = {}
        return tok

    def final_wait_all(self, eng="sp"):
        waits = []
        for e in self.ENG:
            if e != eng and self.cnt[e] > 0:
                waits.append((e, self.cnt[e]))
        for k, c in self.dma_cnt.items():
            if c > 0:
                waits.append((k, c))
        self.ops[eng].append((waits, None, None))

    def simulate(self):
        sem = {}
        pc = {e: 0 for e in self.ENG}
        progress = True
        while progress:
            progress = False
            for e in self.ENG:
                while pc[e] < len(self.ops[e]):
                    waits, fn, inc = self.ops[e][pc[e]]
                    if all(sem.get(k, 0) >= v for k, v in waits):
                        if inc is not None:
                            sem[inc[0]] = sem.get(inc[0], 0) + inc[1]
                        pc[e] += 1
                        progress = True
                    else:
                        break
        stuck = {e: (pc[e], len(self.ops[e])) for e in self.ENG if pc[e] < len(self.ops[e])}
        for e in stuck:
            waits, fn, inc = self.ops[e][pc[e]]
            print("STUCK", e, pc[e], [(k, v, sem.get(k, 0)) for k, v in waits if sem.get(k, 0) < v])
        return not stuck

    def emit(self):
        nc = self.nc
        with ExitStack() as es:
            semh = {}
            for e in self.ENG:
                semh[e] = es.enter_context(nc.semaphore("s_" + e))
            for k in self.dma_cnt:
                semh[k] = es.enter_context(nc.semaphore("s_dma_%s_%d" % (k[1], k[2])))
            block = es.enter_context(nc.Block())
            engmap = {"pe": "tensor", "act": "scalar", "dve": "vector", "pool": "gpsimd", "sp": "sync"}

            def mk(e):
                def body(eng):
                    for waits, fn, inc in self.ops[e]:
                        for key, val in waits:
                            eng.wait_ge(semh[key], val)
                        if fn is None:
                            continue
                        ins = fn(eng)
                        ins.then_inc(semh[inc[0]], inc[1])
                return body
            for e in self.ENG:
                getattr(block, engmap[e])(mk(e))


def build_program():
    nc = bass.Bass("TRN2", target_bir_lowering=False)

    def din(name, shape, dt=F32):
        return nc.dram_tensor(name, list(shape), dt, kind="ExternalInput").ap()

    def dout(name, shape, dt=F32):
        return nc.dram_tensor(name, list(shape), dt, kind="ExternalOutput").ap()

    xc = din("xc", [1024, D])
    xo = din("xo", [1024, D])
    xs = din("xs", [128, D])
    sconv = din("sconv", [16, 30, DC])
    spool = din("spool", [16, 15, DC])
    winb = din("winb", [16, 512, 512])
    flag = din("flag", [128, 1])
    invcnt = din("invcnt", [5, 4, 512])
    ident = din("ident", [128, 128])
    norm_mix = din("norm_mix", [2, D])
    norm_ffn = din("norm_ffn", [2, D])
    norm_final = din("norm_final", [D])
    ab_w_in = din("ab_w_in", [D, 3072])
    ab_conv_w = din("ab_conv_w", [31, DC])
    ab_conv_b = din("ab_conv_b", [DC])
    ab_ln_g = din("ab_ln_g", [DC])
    ab_ln_b = din("ab_ln_b", [DC])
    ab_pool_w = din("ab_pool_w", [4, 256, 256])
    ab_pool_scale = din("ab_pool_scale", [DC])
    ab_w_out = din("ab_w_out", [D, D])
    cd_w_in = din("cd_w_in", [D, CD_IN])
    cd_w_cmp = din("cd_w_cmp", [256])
    cd_w_out = din("cd_w_out", [D, D])
    rel_bias = din("rel_bias", [512])
    caus_c = din("caus_c", [128, 128])
    wm4_c = din("wm4_c", [128, 128])
    dist0_c = din("dist0_c", [128, 128])
    mskblk_c = din("mskblk_c", [128, 16, 32])
    selm_c = din("selm_c", [8, 128, 2, 32])
    flagb = din("flagb", [128, 1])
    fm1 = din("fm1", [128, 1])
    if SAMPLE_ATTN:
        cache2d = {"cmp": din("cache_cmp", [2560 * 128, 512]), "sel": din("cache_sel", [2560 * 128, 512]),
                   "dsa": din("cache_dsa", [2560 * 128, 512])}
        cache_idx2d = din("cache_idx", [2560 * 128, 64])
        pt16 = din("pt16", [256], I32)
        pidx_c = din("pidx_c", [128, 1])
    ffn_w1 = din("ffn_w1", [2, D, DFF])
    ffn_w2 = din("ffn_w2", [2, DFF, D])
    y_o = dout("y_o", [1024, D])
    y_s = dout("y_s", [128, D])
    conv_o = dout("conv_o", [30, DC])
    conv_s = dout("conv_s", [16, 30, DC])
    pool_o = dout("pool_o", [15, DC])
    pool_s = dout("pool_s", [16, 15, DC])
    kvout_o = {"cmp": dout("cmp_o", [1024, 512]), "sel": dout("sel_o", [1024, 512]),
               "win": dout("win_o", [1024, 512]), "dsa": dout("dsa_o", [1024, 512]),
               "idx": dout("idx_o", [1024, 64])}
    kvout_s = {"cmp": dout("cmp_s", [128, 512]), "sel": dout("sel_s", [128, 512]),
               "dsa": dout("dsa_s", [128, 512]), "idx": dout("idx_s", [128, 64])}
    win_s = dout("win_s", [16, 512, 512])
    KT_s = {nm: nc.dram_tensor("KTs_" + nm, [2, 128, 2048], BF16, kind="ExternalOutput").ap() for nm in ("cmp", "sel", "win", "dsa")}
    V_s = {nm: nc.dram_tensor("Vs_" + nm, [2048, 256], BF16, kind="ExternalOutput").ap() for nm in ("cmp", "sel", "win", "dsa")}
    KI_s = nc.dram_tensor("KIs", [64, 2048], BF16, kind="ExternalOutput").ap()
    BT_s = nc.dram_tensor("BTs", [2, 128, 16, 128], F32, kind="ExternalOutput").ap()

    with ExitStack() as es:
        def sb(name, shape, dt=F32):
            return es.enter_context(nc.sbuf_tensor(name, list(shape), dt))

        def pst(name, shape, dt=F32):
            return es.enter_context(nc.psum_tensor(name, list(shape), dt))

        S = Sched(nc)

        IDF = sb("IDF", [128, 128])
        IDB = sb("IDB", [128, 128], BF16)
        ONES = sb("ONES", [128, 128])
        EPSC = sb("EPSC", [128, 1])
        FLAG = sb("FLAG", [128, 1])
        GC = sb("GC", [128, 4, 16])
        CW = sb("CW", [128, 8, 31])
        CB = sb("CB", [128, 8])
        LG = sb("LG", [128, 8])
        LB = sb("LB", [128, 8])
        PSC = sb("PSC", [128, 8])
        PW = sb("PW", [128, 4, 2, 256], BF16)
        X = sb("X", [128, 4, D])
        XN = [sb("XN%d" % i, [128, D], BF16) for i in range(1)]
        SS = sb("SS", [128, 16])
        RS = sb("RS", [128, 16])
        HT = sb("HT", [128, 16, 512], BF16)
        WP = [sb("WP%d" % i, [128, 8192], BF16) for i in range(4)]
        UE = [sb("UE%d" % i, [128, 608]) for i in range(2)]
        VE = [sb("VE%d" % i, [128, 528]) for i in range(2)]
        PA_ = [sb("PLA%d" % i, [128, 528]) for i in range(2)]
        UHP = sb("UHP", [128, 8, 30])
        VHP = sb("VHP", [128, 8, 15])
        CT = sb("CT", [128, 8, 512])
        DT = sb("DT", [128, 2, 512], BF16)
        YCP = sb("YCP", [128, 16, 512], BF16)
        GT = sb("GT", [128, 4, 512], BF16)
        WK = [sb("WK%d" % i, [128, 512]) for i in range(4)]
        MEAN = sb("MEAN", [128, 512])
        RSTD = sb("RSTD", [128, 512])
        STC = sb("STC", [120, 4, 128])
        STO = sb("STO", [120, 4, 128])
        OUTS = sb("OUTS", [128, 480])
        STG = [sb("STG%d" % i, [128, 512]) for i in range(2)]

        B31 = sb("B31", [128, 16])
        FLAGB = sb("FLAGB", [128, 1])
        FM1 = sb("FM1", [128, 1])
        WCB = sb("WCB", [128, 256])
        WBm = sb("WBm", [128, 2, 16, 32], BF16)
        CKT = sb("CKT", [128, 2, 32], BF16)
        CV = sb("CV", [32, 2, 128], BF16)
        G = sb("G", [128, 4, 24])
        WI = sb("WI", [128, 4, 8])
        MB = sb("MB", [128, 2, 32])
        IMP = sb("IMP", [128, 32])
        M12 = [sb("M12_%d" % i, [128, 2, 32]) for i in range(2)]
        SM = sb("SM", [128, 64])
        SCC = [sb("SCC%d" % i, [128, 4, 32]) for i in range(2)]
        PNB = sb("PNB", [128, 4, 32], BF16)
        PCT = sb("PCT", [32, 4, 128], BF16)
        T8 = sb("T8", [128, 16])
        CAUS = sb("CAUS", [128, 128])
        WM4 = sb("WM4", [128, 128])
        BCS = sb("BCS", [128, 2, 2, 8])
        VST = sb("VST", [128, 2, 256], BF16)

        PS = [pst("PS%d" % i, [128, 512]) for i in range(6)]
        TB = [pst("TB%d" % i, [128, 1024], BF16) for i in range(2)]

        CUT = int(os.environ.get("KCUT", "0"))
        if os.environ.get("KDEBUG"):
            print("sbuf remaining after alloc", nc.sbuf_bytes_remaining)
        rr = {"wp2": 0, "sm": 0, "kv": 0, "m12": 0, "bt": 0, "vst": 0, "ps": 0, "wk": 0, "wp": 0, "stat": 0, "stg": 0, "xn": 0, "ue": 0, "ve": 0, "ev": 0}

        def nxt(key, n):
            v = rr[key] % n
            rr[key] += 1
            return v

        def psum():
            i = nxt("ps", 6)
            return PS[i], ("PS", i)

        def wk():
            i = nxt("wk", 4)
            return WK[i], ("WK", i)

        def dma(eng, out, in_, reads, writes, **kw):
            S.op(eng, lambda e: e.dma_start(out=out, in_=in_, **kw), reads, writes, dma=True)

        def mm(out, lhsT, rhs, start, stop, reads, writes):
            S.op("pe", lambda e: e.matmul(out, lhsT, rhs, start=start, stop=stop), reads, writes)

        def tr(out, in_, idn, reads, writes):
            S.op("pe", lambda e: e.transpose(out, in_, idn), reads, writes)

        def act(out, in_, func, reads, writes, **kw):
            S.op("act", lambda e: e.activation(out, in_, func, **kw), reads, writes)

        def tt(eng, out, in0, in1, op, reads, writes):
            S.op(eng, lambda e: e.tensor_tensor(out=out, in0=in0, in1=in1, op=op), reads, writes)

        def ts(eng, out, in0, s1, s2, op0, op1, reads, writes):
            if s2 is None:
                S.op(eng, lambda e: e.tensor_scalar(out=out, in0=in0, scalar1=s1, scalar2=None, op0=op0), reads, writes)
            else:
                S.op(eng, lambda e: e.tensor_scalar(out=out, in0=in0, scalar1=s1, scalar2=s2, op0=op0, op1=op1), reads, writes)

        def stt(eng, out, in0, scalar, in1, op0, op1, reads, writes):
            S.op(eng, lambda e: e.scalar_tensor_tensor(out=out, in0=in0, scalar=scalar, in1=in1, op0=op0, op1=op1), reads, writes)

        def cp(eng, out, in_, reads, writes):
            S.op("act", lambda e: e.copy(out, in_), reads, writes)

        def recip(out, in_, reads, writes):
            S.op("dve", lambda e: e.reciprocal(out, in_), reads, writes)

        def memset(eng, ap, val, writes):
            S.op(eng, lambda e: e.memset(ap, val), (), writes)

        def evac_eng():
            return "act" if nxt("ev", 2) == 0 else "dve"

        wp_restrict = [False]

        def load_w(src):
            i = nxt("wp2", 2) if wp_restrict[0] else nxt("wp", 4)
            k, c = src.shape[1], src.shape[2]
            view = WP[i][:, 0:k * c].rearrange("p (k c) -> p k c", k=k)
            dma("pool", view, src, [], [("WP", i)])
            return view, ("WP", i)

        dma("sp", IDF[:], ident[:, :], [], ["IDF"])
        dma("pool", IDB[:], ident[:, :], [], ["IDB"])
        memset("pool", ONES[:], 1.0 / DC, ["ONES"])
        memset("pool", EPSC[:], EPS, ["EPSC"])
        memset("pool", UHP[:], 0.0, [("UH", c) for c in range(8)])
        memset("pool", VHP[:], 0.0, [("VH", c) for c in range(8)])
        dma("sp", FLAG[:], flag[:, :], [], ["FLAG"])
        for i, src in enumerate([norm_mix[0], norm_ffn[0], norm_mix[1], norm_ffn[1]]):
            dma("sp", GC[:, i, :], src.rearrange("(c p) -> p c", p=128), [], ["GC"], allow_slow_non_contiguous=True)
        for dst, src, nm in ((CB, ab_conv_b, "CB"), (LG, ab_ln_g, "LG"), (LB, ab_ln_b, "LB"), (PSC, ab_pool_scale, "PSC")):
            dma("sp", dst[:], src.rearrange("(c p) -> p c", p=128), [], [nm], allow_slow_non_contiguous=True)
        for hh in range(2):
            dma("sp", STG[hh][0:31, :], ab_conv_w[:, hh * 512:(hh + 1) * 512], [], [("STG", hh)])
        for c in range(8):
            p, pn = psum()
            tr(p[:, 0:31], STG[c // 4][0:31, (c % 4) * 128:(c % 4 + 1) * 128], IDF[0:31, 0:31], [("STG", c // 4), "IDF"], [pn])
            cp("act", CW[:, c, :], p[:, 0:31], [pn], ["CW"])
        dma("pool", PW[:], ab_pool_w.rearrange("g (i p) o -> p g i o", p=128), [], ["PW"])

        def load_x(src, ntl):
            for t in range(ntl):
                dma("sp", X[:, t, :], src[t * 128:(t + 1) * 128, :], [], [("X", t)])

        def norm_to_ht(gidx, ntl):
            for t in range(ntl):
                c = nxt("stat", 16)
                xi = nxt("xn", 1)
                act(XN[xi][:], X[:, t, :], AF.Square, [("X", t)], [("XN", xi), ("SS", c)], accum_out=SS[:, c:c + 1])
                act(RS[:, c:c + 1], SS[:, c:c + 1], AF.Sqrt, [("SS", c), "EPSC"], [("RS", c)], scale=1.0 / D, bias=EPSC[:, 0:1])
                recip(RS[:, c:c + 1], RS[:, c:c + 1], [("RS", c)], [("RS", c)])
                act(XN[xi][:], X[:, t, :], AF.Identity, [("X", t), ("RS", c)], [("XN", xi)], scale=RS[:, c:c + 1])
                for half in range(2):
                    tb = TB[half]
                    for k in range(8):
                        kk = half * 8 + k
                        tr(tb[:, k * 128:(k + 1) * 128], XN[xi][:, kk * 128:(kk + 1) * 128], IDB[:], [("XN", xi), "IDB"], [("TB", half)])
                    tt("dve", HT[:, half * 8:half * 8 + 8, t * 128:(t + 1) * 128],
                       tb[:].rearrange("p (c n) -> p c n", c=8),
                       GC[:, gidx, half * 8:half * 8 + 8].unsqueeze(2).to_broadcast([128, 8, 128]),
                       ALU.mult, [("TB", half), "GC"], [("HT", t)])

        def wblk(w2d, c0, ncol=512):
            return w2d[:, c0:c0 + ncol].rearrange("(k p) c -> p k c", p=128)

        def l0_mixer(kind, pp, NP, ntl):
            smp = kind == "smp"
            htr = [("HT", t) for t in range(ntl)]
            blocks = [0, 1024, 512, 1536, 2048, 2560]
            loaded = {}

            def ensure(bi):
                if bi < len(blocks) and bi not in loaded:
                    loaded[bi] = load_w(wblk(ab_w_in, blocks[bi]))
            ensure(0); ensure(1); ensure(2)
            if kind == "own" and pp == 0:
                ts("dve", UHP[:], UHP[:], FLAG[:, 0:1], None, ALU.mult, ALU.bypass, [("UH", c) for c in range(8)] + ["FLAG"], [("UH", c) for c in range(8)])
                ts("dve", VHP[:], VHP[:], FLAG[:, 0:1], None, ALU.mult, ALU.bypass, [("VH", c) for c in range(8)] + ["FLAG"], [("VH", c) for c in range(8)])
            pidx = {"ctx": 0, "own": 2, "smp": 4}[kind] + pp
            for cc in range(8):
                i, j = cc // 4, cc % 4
                ensure(2 * i + 1); ensure(2 * i + 2); ensure(2 * i + 3)
                Wa, wan = loaded[2 * i]
                Wg, wgn = loaded[2 * i + 1]
                pa, pan = psum()
                pg, pgn = psum()
                for k in range(16):
                    mm(pa[:, :NP], Wa[:, k, j * 128:(j + 1) * 128], HT[:, k, :NP], k == 0, k == 15, htr + [wan], [pan])
                for k in range(16):
                    mm(pg[:, :NP], Wg[:, k, j * 128:(j + 1) * 128], HT[:, k, :NP], k == 0, k == 15, htr + [wgn], [pgn])
                sg, sgn = wk()
                act(sg[:, :NP], pg[:, :NP], AF.Sigmoid, [pgn], [sgn])
                ei = nxt("ue", 2)
                uen = ("UE", ei)
                if not smp:
                    ue = UE[ei][:, 0:30 + NP]
                    cp("act", ue[:, 0:30], UHP[:, cc, :], [("UH", cc)], [uen])
                    tt("dve", ue[:, 30:30 + NP], pa[:, :NP], sg[:, :NP], ALU.mult, [pan, sgn], [uen])
                    cp("act", UHP[:, cc, :], ue[:, NP:NP + 30], [uen], [("UH", cc)])
                    ext = lambda jj: ue[:, jj:jj + NP]
                    acc = CT[:, cc, :NP]
                else:
                    ue3 = UE[ei][:, 0:608].rearrange("p (b t) -> p b t", t=38)
                    dma("sp", STC[:], sconv.rearrange("(q b) t c -> (b t) q c", q=4)[:, :, cc * 128:(cc + 1) * 128], [], ["STC"])
                    ph, phn = psum()
                    for q in range(4):
                        tr(ph[:, q * 120:(q + 1) * 120], STC[:, q, :], IDF[0:120, 0:120], ["STC", "IDF"], [phn])
                    cp("act", ue3[:, :, 0:30], ph[:, 0:480].rearrange("p (b t) -> p b t", t=30), [phn], [uen])
                    tt("dve", ue3[:, :, 30:38], pa[:, :NP].rearrange("p (b t) -> p b t", t=8),
                       sg[:, :NP].rearrange("p (b t) -> p b t", t=8), ALU.mult, [pan, sgn], [uen])
                    ext = lambda jj: ue3[:, :, jj:jj + 8]
                    acc = CT[:, cc, :NP].rearrange("p (b t) -> p b t", t=8)
                ctn = ("CT", cc)
                ts("dve", acc, ext(0), CW[:, cc, 0:1], CB[:, cc:cc + 1], ALU.mult, ALU.add, [uen, "CW", "CB"], [ctn])
                for jj in range(1, 31):
                    stt("dve", acc, ext(jj), CW[:, cc, jj:jj + 1], acc, ALU.mult, ALU.add, [uen, "CW", ctn], [ctn])
                if kind == "own" and pp == 1:
                    p, pn = psum()
                    tr(p[0:30, 0:128], ue[:, NP:NP + 30], IDF[:], [uen, "IDF"], [pn])
                    cp("act", STO[0:30, 0, :], p[0:30, 0:128], [pn], ["STO"])
                    dma("sp", conv_o[:, cc * 128:(cc + 1) * 128], STO[0:30, 0, :], ["STO"], [])
                if smp:
                    cp("act", OUTS[:, 0:480].rearrange("p (b t) -> p b t", t=30), ue3[:, :, 8:38], [uen], ["OUTS"])
                    p, pn = psum()
                    for q in range(4):
                        tr(p[0:120, q * 128:(q + 1) * 128], OUTS[:, q * 120:(q + 1) * 120], IDF[:], ["OUTS", "IDF"], [pn])
                    cp("act", STO[:], p[0:120, :].rearrange("p (q c) -> p q c", q=4), [pn], ["STO"])
                    dma("sp", conv_s.rearrange("(q b) t c -> (b t) q c", q=4)[:, :, cc * 128:(cc + 1) * 128], STO[:], ["STO"], [])
            p1, p1n = psum()
            p2, p2n = psum()
            for cc in range(8):
                sq, sqn = wk()
                act(sq[:, :NP], CT[:, cc, :NP], AF.Square, [("CT", cc)], [sqn])
                mm(p1[:, :NP], ONES[:], CT[:, cc, :NP], cc == 0, cc == 7, ["ONES", ("CT", cc)], [p1n])
                mm(p2[:, :NP], ONES[:], sq[:, :NP], cc == 0, cc == 7, ["ONES", sqn], [p2n])
            cp("act", MEAN[:, :NP], p1[:, :NP], [p1n], ["MEAN"])
            msq, msqn = wk()
            tt("dve", msq[:, :NP], MEAN[:, :NP], MEAN[:, :NP], ALU.mult, ["MEAN"], [msqn])
            tt("dve", msq[:, :NP], p2[:, :NP], msq[:, :NP], ALU.subtract, [p2n, msqn], [msqn])
            act(RSTD[:, :NP], msq[:, :NP], AF.Sqrt, [msqn, "EPSC"], ["RSTD"], bias=EPSC[:, 0:1])
            recip(RSTD[:, :NP], RSTD[:, :NP], ["RSTD"], ["RSTD"])
            for cc in range(8):
                t1, t1n = wk()
                tt("dve", t1[:, :NP], CT[:, cc, :NP], MEAN[:, :NP], ALU.subtract, [("CT", cc), "MEAN"], [t1n])
                tt("dve", t1[:, :NP], t1[:, :NP], RSTD[:, :NP], ALU.mult, [t1n, "RSTD"], [t1n])
                act(YCP[:, cc, :NP], t1[:, :NP], AF.Silu, [t1n, "LG", "LB"], [("YCP", cc)], scale=LG[:, cc:cc + 1], bias=LB[:, cc:cc + 1])
            for cc in range(8):
                i, j = cc // 4, cc % 4
                ensure(4 + i); ensure(5)
                Wv, wvn = loaded[4 + i]
                pv, pvn = psum()
                for k in range(16):
                    mm(pv[:, :NP], Wv[:, k, j * 128:(j + 1) * 128], HT[:, k, :NP], k == 0, k == 15, htr + [wvn], [pvn])
                ei = nxt("ve", 2)
                ven = ("VE", ei)
                g = cc // 2
                w = POOL_WINDOWS[g]
                ic, icn = wk()
                dma("sp", ic[:, :NP], invcnt[pidx, g, 0:NP].partition_broadcast(128), [], [icn])
                if not smp:
                    L = 15 + NP
                    ve = VE[ei][:, 0:L]
                    cp("act", ve[:, 0:15], VHP[:, cc, :], [("VH", cc)], [ven])
                    cp("act", ve[:, 15:L], pv[:, :NP], [pvn], [ven])
                    cp("act", VHP[:, cc, :], ve[:, NP:NP + 15], [ven], [("VH", cc)])
                    sl = lambda buf, a, b: buf[:, a:b]
                    bufs = [PA_[0][:, 0:L], PA_[1][:, 0:L]]
                    fin = lambda buf: buf[:, 15:L]
                    dview = DT[:, cc % 2, :NP]
                    icv = ic[:, :NP]
                else:
                    L = 23
                    ve = VE[ei][:, 0:368].rearrange("p (b t) -> p b t", t=23)
                    dma("sp", STC[:, 0:2, :], spool.rearrange("(q b) t c -> (b t) q c", q=2)[:, :, cc * 128:(cc + 1) * 128], [], ["STC"])
                    ph, phn = psum()
                    for q in range(2):
                        tr(ph[:, q * 120:(q + 1) * 120], STC[:, q, :], IDF[0:120, 0:120], ["STC", "IDF"], [phn])
                    cp("act", ve[:, :, 0:15], ph[:, 0:240].rearrange("p (b t) -> p b t", t=15), [phn], [ven])
                    cp("act", ve[:, :, 15:23], pv[:, :NP].rearrange("p (b t) -> p b t", t=8), [pvn], [ven])
                    sl = lambda buf, a, b: buf[:, :, a:b]
                    bufs = [PA_[0][:, 0:368].rearrange("p (b t) -> p b t", t=23), PA_[1][:, 0:368].rearrange("p (b t) -> p b t", t=23)]
                    fin = lambda buf: buf[:, :, 15:23]
                    dview = DT[:, cc % 2, :NP].rearrange("p (b t) -> p b t", t=8)
                    icv = ic[:, :NP].rearrange("p (b t) -> p b t", t=8)
                cur, curn = ve, ven
                sh = 1
                bi = 0
                while sh < w:
                    nb_, nbn = bufs[bi], ("PLA", bi)
                    lo = 2 * sh - 1
                    tt("dve", sl(nb_, lo, L), sl(cur, lo, L), sl(cur, lo - sh, L - sh), ALU.add, [curn], [nbn])
                    cur, curn = nb_, nbn
                    sh *= 2
                    bi ^= 1
                tm, tmn = wk()
                tmv = tm[:, :NP] if not smp else tm[:, :NP].rearrange("p (b t) -> p b t", t=8)
                tt("dve", tmv, fin(cur), icv, ALU.mult, [curn, icn], [tmn])
                tt("dve", dview, tmv, fin(ve), ALU.subtract, [tmn, ven], [("DT", cc % 2)])
                if kind == "own" and pp == 1:
                    p, pn = psum()
                    tr(p[0:15, 0:128], ve[:, NP:NP + 15], IDF[:], [ven, "IDF"], [pn])
                    cp("act", STO[0:15, 0, :], p[0:15, 0:128], [pn], ["STO"])
                    dma("sp", pool_o[:, cc * 128:(cc + 1) * 128], STO[0:15, 0, :], ["STO"], [])
                if smp:
                    cp("act", OUTS[:, 0:240].rearrange("p (b t) -> p b t", t=15), ve[:, :, 8:23], [ven], ["OUTS"])
                    p, pn = psum()
                    for q in range(2):
                        tr(p[0:120, q * 128:(q + 1) * 128], OUTS[:, q * 120:(q + 1) * 120], IDF[:], ["OUTS", "IDF"], [pn])
                    cp("act", STO[:, 0:2, :], p[0:120, 0:256].rearrange("p (q c) -> p q c", q=2), [pn], ["STO"])
                    dma("sp", pool_s.rearrange("(q b) t c -> (b t) q c", q=2)[:, :, cc * 128:(cc + 1) * 128], STO[:, 0:2, :], ["STO"], [])
                if cc % 2 == 1:
                    for oc in range(2):
                        p, pn = psum()
                        for icc in range(2):
                            mm(p[:, :NP], PW[:, g, icc, oc * 128:(oc + 1) * 128], DT[:, icc, :NP], icc == 0, icc == 1,
                               ["PW", ("DT", icc)], [pn])
                        ch = 2 * g + oc
                        act(YCP[:, 8 + ch, :NP], p[:, :NP], AF.Identity, [pn, "PSC"], [("YCP", 8 + ch)], scale=PSC[:, ch:ch + 1])

        def proj_residual(w2d, ntl, lhs_buf, lhs_name, nk):
            lr = [(lhs_name, k) for k in range(nk)]
            nxtw = load_w(wblk(w2d, 0))
            for nb in range(4):
                W, wn = nxtw
                if nb < 3:
                    nxtw = load_w(wblk(w2d, (nb + 1) * 512))
                for t in range(ntl):
                    p, pn = psum()
                    for k in range(nk):
                        mm(p[:, :], lhs_buf[:, k, t * 128:(t + 1) * 128], W[:, k, :], k == 0, k == nk - 1, lr + [wn], [pn])
                    tt("dve", X[:, t, nb * 512:(nb + 1) * 512], X[:, t, nb * 512:(nb + 1) * 512], p[:, :], ALU.add, [pn, ("X", t)], [("X", t)])

        def ffn(layer, NP, ntl):
            htr = [("HT", t) for t in range(ntl)]
            w1 = ffn_w1[layer]
            w2 = ffn_w2[layer]

            def ld(fb):
                a = load_w(wblk(w1, fb * 512))
                b = load_w(w2[fb * 512:(fb + 1) * 512, :].rearrange("(k p) c -> p k c", p=128))
                return a, b
            nxtw = ld(0)
            for fb in range(16):
                (W1, w1n), (W2, w2n) = nxtw
                if fb < 15:
                    nxtw = ld(fb + 1)
                for j in range(4):
                    p, pn = psum()
                    for k in range(16):
                        mm(p[:, :NP], W1[:, k, j * 128:(j + 1) * 128], HT[:, k, :NP], k == 0, k == 15, htr + [w1n], [pn])
                    rl, rln = wk()
                    act(rl[:, :NP], p[:, :NP], AF.Relu, [pn], [rln])
                    tt("dve", GT[:, j, :NP], rl[:, :NP], rl[:, :NP], ALU.mult, [rln], [("GT", j)])
                gtr = [("GT", j) for j in range(4)]
                for t in range(ntl):
                    for nb in range(4):
                        p, pn = psum()
                        for j in range(4):
                            mm(p[:, :], GT[:, j, t * 128:(t + 1) * 128], W2[:, j, nb * 512:(nb + 1) * 512], j == 0, j == 3, gtr + [w2n], [pn])
                        tt("dve", X[:, t, nb * 512:(nb + 1) * 512], X[:, t, nb * 512:(nb + 1) * 512], p[:, :], ALU.add, [pn, ("X", t)], [("X", t)])

        def l1_kv_proj(kind, pp, NP, ntl):
            htr = [("HT", t) for t in range(ntl)]
            specs = [("cmp", C_KVC, 512), ("sel", C_KVS, 512), ("win", C_KVW, 512), ("dsa", C_KVD, 512), ("idx", C_KI, 64)]
            col0 = (0 if kind != "own" else 1024) + pp * 512
            nxtw = load_w(wblk(cd_w_in, specs[0][1], specs[0][2]))
            for si, (nm, c0, ncol) in enumerate(specs):
                W, wn = nxtw
                if si + 1 < len(specs):
                    nxtw = load_w(wblk(cd_w_in, specs[si + 1][1], specs[si + 1][2]))
                for t in range(ntl):
                    p, pn = psum()
                    for k in range(16):
                        mm(p[:, 0:ncol], HT[:, k, t * 128:(t + 1) * 128], W[:, k, :], k == 0, k == 15, htr + [wn], [pn])
                    if kind != "ctx":
                        si_ = nxt("stg", 2)
                        stg, stn = STG[si_], ("STG", si_)
                        cp("act", stg[:, 0:ncol], p[:, 0:ncol], [pn], [stn])
                        if kind == "own":
                            r0 = pp * 512 + t * 128
                            dma("sp", kvout_o[nm][r0:r0 + 128, :], stg[:, 0:ncol], [stn], [])
                        elif nm == "win":
                            for b in range(16):
                                dma("sp", win_s[b, 504:512, :], stg[b * 8:(b + 1) * 8, 0:512], [stn], [])
                        else:
                            dma("sp", kvout_s[nm][:, :], stg[:, 0:ncol], [stn], [])
                    if nm != "idx" and CUT != 7:
                        vi = nxt("vst", 2)
                        cp("dve", VST[:, vi, :], p[:, 256:512], [pn], [("VST", vi)])
                        dma("sp", V_s[nm][col0 + t * 128:col0 + (t + 1) * 128, :], VST[:, vi, :], [("VST", vi)], [("Vs", nm)])
                if CUT != 7:
                    ngr = 2 if nm != "idx" else 1
                    for g in range(ngr):
                        mrows = 128 if nm != "idx" else 64
                        p, pn = psum()
                        for k in range(16):
                            mm(p[0:mrows, :NP], W[:, k, g * 128:g * 128 + mrows], HT[:, k, :NP], k == 0, k == 15, htr + [wn], [pn])
                        di = nxt("vst", 2)
                        cp(evac_eng(), DT[0:mrows, di, :NP], p[0:mrows, :NP], [pn], [("DT", di)])
                        if nm != "idx":
                            dma("sp", KT_s[nm][g, :, col0:col0 + NP], DT[:, di, :NP], [("DT", di)], [("KTs", nm)])
                        else:
                            dma("sp", KI_s[:, col0:col0 + NP], DT[0:64, di, :NP], [("DT", di)], ["KIs"])

        SCALE = 128.0 ** -0.5
        NEG = -1e30
        Q16 = WP[2][:, :].rearrange("p (k c) -> p k c", k=16)
        KTb = [WP[3][:, s_ * 4096:s_ * 4096 + 2048] for s_ in range(2)]
        Vb = [WP[3][:, s_ * 4096 + 2048:(s_ + 1) * 4096].rearrange("p (t d) -> p t d", t=16) for s_ in range(2)]
        HTf = HT[:, :, :].rearrange("p k c -> p (k c)")
        Pb = HTf[:, 0:2048]
        PTb = HTf[:, 2048:4096].rearrange("p (t q) -> p t q", t=16)
        OB = HTf[:, 4096:6144]
        JK = HTf[:, 6144:8192]
        CTf = CT[:, :, :].rearrange("p c n -> p (c n)")
        SC0, SC0n = CTf[:, 0:2048], [("CT", c) for c in range(4)]
        SC1, SC1n = CTf[:, 2048:4096], [("CT", c) for c in range(4, 8)]
        OACC = [STG[0], STG[1], MEAN, RSTD]
        OACCn = [("STG", 0), ("STG", 1), "MEAN", "RSTD"]
        BC = UE[0][:, 0:256].rearrange("p (h n) -> p h n", h=8)
        MK01 = UE[0][:, 256:512].rearrange("p (h n) -> p h n", h=8)
        BCn = ("UE", 0)
        WCBv = WCB[:, :].rearrange("p (c j g) -> p c j g", c=2, j=64)

        BUF_P = {"SC0": SC0, "SC0n": SC0n, "Pb": Pb, "PTb": PTb}

        def smcol(n=1):
            c = rr["sm"] % 60
            if c + n > 60:
                c = 0
            rr["sm"] = c + n
            return SM[:, c:c + n], [("SM", cc) for cc in range(c, c + n)]

        def oacc(h):
            return OACC[h // 4][:, (h % 4) * 128:(h % 4 + 1) * 128], OACCn[h // 4]

        def bucket_thresholds():
            d = np.arange(0, 400)
            nf = np.maximum(d, 1).astype(np.float32)
            big = 16 + (np.log(nf / np.float32(16)) / np.float32(np.log(128 / 16)) * np.float32(16)).astype(np.int32)
            bk = np.where(d < 16, d, np.minimum(big, 31))
            return [int(np.min(d[bk >= b])) for b in range(1, 32)]

        def attn_setup():
            dma("sp", FLAGB[:], flagb[:, :], [], ["FLAGB"])
            dma("sp", FM1[:], fm1[:, :], [], ["FM1"])
            dma("sp", CAUS[:], caus_c[:, :], [], ["CAUS"])
            dma("sp", WM4[:], wm4_c[:, :], [], ["WM4"])
            dma("sp", WCB[:], cd_w_cmp.partition_broadcast(128), [], ["WCB"])
            rbb, rbn = wk()
            dma("sp", rbb[:, :], rel_bias.partition_broadcast(128), [], [rbn])
            rb3 = rbb[:, :].rearrange("p (b h) -> p b h", b=32)
            cp("dve", B31[:], rb3[:, 31, :], [rbn], ["B31"])
            dl, dln = wk()
            dl3 = dl[:, 0:496].rearrange("p (b h) -> p b h", b=31)
            tt("dve", dl3, rb3[:, 1:32, :], rb3[:, 0:31, :], ALU.subtract, [rbn], [dln])
            d0, d0n = wk()
            dma("sp", d0[:, 0:128], dist0_c[:, :], [], [d0n])
            ge, gen = wk()
            thr = bucket_thresholds()
            acc3 = SC0.rearrange("p (h j) -> p h j", h=16)
            tmp3 = SC1.rearrange("p (h j) -> p h j", h=16)
            for kind_ in range(2):
                off = 128.0 * kind_
                tt("dve", acc3, rb3[:, 0, :].unsqueeze(2).to_broadcast([128, 16, 128]),
                   B31[:, :].unsqueeze(2).to_broadcast([128, 16, 128]), ALU.subtract, [rbn, "B31"], SC0n)
                for b in range(1, 32):
                    ts("dve", ge[:, 0:128], d0[:, 0:128], float(thr[b - 1]) - off, None, ALU.is_ge, None, [d0n], [gen])
                    tt("dve", tmp3, ge[:, 0:128].unsqueeze(1).to_broadcast([128, 16, 128]),
                       dl3[:, b - 1, :].unsqueeze(2).to_broadcast([128, 16, 128]), ALU.mult, [gen, dln], SC1n)
                    tt("dve", acc3, acc3, tmp3, ALU.add, SC0n + SC1n, SC0n)
                if kind_ == 0:
                    tt("dve", acc3, acc3, CAUS[:, :].unsqueeze(1).to_broadcast([128, 16, 128]), ALU.add, SC0n + ["CAUS"], SC0n)
                dma("sp", BT_s[kind_], acc3, SC0n, ["BTs"])
            for kind_ in range(2):
                for jj in range(2):
                    col = 63 + 64 * jj
                    dma("sp", BCS[:, kind_, jj, :], BT_s[kind_, :, 0:8, col], ["BTs"], ["BCS"], allow_slow_non_contiguous=True)
            mk, mkn = wk()
            dma("sp", mk[:, :].rearrange("p (i n) -> p i n", i=16), mskblk_c[:, :, :], [], [mkn])
            wcol, wcn = smcol(2)
            for g in range(2):
                for hf in range(2):
                    dma("sp", wcol[hf * 64:(hf + 1) * 64, g:g + 1], cd_w_cmp[128 + g:256:2].rearrange("(j o) -> j o", o=1), [], wcn,
                        allow_slow_non_contiguous=True)
            for g in range(2):
                ts("dve", WBm[:, g, :, :], mk[:, :].rearrange("p (i n) -> p i n", i=16), wcol[:, g:g + 1], None, ALU.mult, None, [mkn] + wcn, ["WBm"])

        def load_kv(nm, g, c0, c1):
            s_ = nxt("kv", 2)
            kvn = ("KV", s_)
            dma("sp", KTb[s_][:, 0:c1 - c0], KT_s[nm][g, :, c0:c1], [("KTs", nm)], [kvn])
            dma("sp", Vb[s_][:, 0:(c1 - c0) // 128, :], V_s[nm][c0:c1, g * 128:(g + 1) * 128].rearrange("(t p) d -> p t d", p=128),
                [("Vs", nm)], [kvn])
            return KTb[s_], Vb[s_], kvn

        def dense_head(qc, qt_g, hq, hb, KT, V, kvn, gt0, ntile, mask_fn, gate_ap, gate_names, first, win,
                       nq=128, B=None, noflag=False, qap=None):
            L = 128 * ntile
            gq = 8 + qt_g
            if B is None:
                SC0, SC0n, Pb, PTb = BUF_P["SC0"], BUF_P["SC0n"], BUF_P["Pb"], BUF_P["PTb"]
            else:
                SC0, SC0n, Pb, PTb = B["SC0"], B["SC0n"], B["Pb"], B["PTb"]
            if qap is None:
                qap = Q16[:, hq, qc]
            R = slice(0, nq)
            bi = nxt("bt", 2)
            btn = ("PLA", bi)
            bt = PA_[bi][:, 0:256].rearrange("p (k j) -> p k j", k=2)
            dma("sp", bt, BT_s[:, :, hb, :].rearrange("k q j -> q k j"), ["BTs"], [btn])
            nch = (L + 511) // 512
            for c in range(nch):
                w_ = min(512, L - c * 512)
                p, pn = psum()
                mm(p[R, 0:w_], qap, KT[:, c * 512:c * 512 + w_], True, True, ["Q16", kvn], [pn])
                act(SC0[R, c * 512:c * 512 + w_], p[R, 0:w_], AF.Identity, [pn, "B31"], SC0n, scale=SCALE, bias=B31[R, hb:hb + 1])
            nctx = 0 if noflag else max(0, min(8 - gt0, ntile))
            if nctx > 0:
                ts("dve", SC0[R, 0:nctx * 128], SC0[R, 0:nctx * 128], FLAGB[R, 0:1], None, ALU.add, None, SC0n + ["FLAGB"], SC0n)
            for i in range(ntile):
                dlt = gq - (gt0 + i)
                if dlt == 0 or dlt == 1:
                    tt("dve", SC0[R, i * 128:(i + 1) * 128], SC0[R, i * 128:(i + 1) * 128], bt[R, dlt, :], ALU.add, SC0n + [btn], SC0n)
                if win and dlt == 4:
                    tt("dve", SC0[R, i * 128:(i + 1) * 128], SC0[R, i * 128:(i + 1) * 128], WM4[R, :], ALU.add, SC0n + ["WM4"], SC0n)
            if mask_fn is not None:
                mask_fn(L)
            st, stn = smcol(4)
            S.op("dve", lambda e: e.reduce_max(out=st[R, 0:1], in_=SC0[R, 0:L], axis=AX.X), SC0n, stn)
            ts("dve", st[R, 1:2], st[R, 0:1], -1.0, None, ALU.mult, None, stn, stn)
            act(Pb[R, 0:L], SC0[R, 0:L], AF.Exp, SC0n + stn, ["P"] + stn, bias=st[R, 1:2], accum_out=st[R, 2:3])
            recip(st[R, 3:4], st[R, 2:3], stn, stn)
            if gate_ap is not None:
                tt("dve", st[R, 3:4], st[R, 3:4], gate_ap, ALU.mult, stn + gate_names, stn)
            for i in range(ntile):
                hf = (i // 8) % 2
                tr(TB[hf][:, (i % 8) * nq:(i % 8 + 1) * nq], Pb[R, i * 128:(i + 1) * 128], IDB[R, R], ["P", "IDB"], [("TB", hf)])
                if i % 8 == 7 or i == ntile - 1:
                    n8 = i % 8 + 1
                    i0 = i - (n8 - 1)
                    cp(evac_eng(), PTb[:, i0:i0 + n8, R], TB[hf][:, 0:n8 * nq].rearrange("p (t q) -> p t q", t=n8), [("TB", hf)], ["PT"])
            po, pon = psum()
            for i in range(ntile):
                mm(po[R, 0:128], PTb[:, i, R], V[:, i, :], i == 0, i == ntile - 1, ["PT", kvn], [pon])
            oa, oan = oacc(hb)
            if first:
                ts("dve", oa[R, :], po[R, 0:128], st[R, 3:4], None, ALU.mult, None, [pon] + stn, [oan])
            else:
                stt("dve", oa[R, :], po[R, 0:128], st[R, 3:4], oa[R, :], ALU.mult, ALU.add, [pon, oan] + stn, [oan])

        def attn_prompt(pp, final_fn):
            NP = 512
            htr = [("HT", t) for t in range(4)]
            S.alias(["Q16"], [("WP", 2)])
            S.alias([("KV", 0), ("KV", 1)], [("WP", 3)])
            wp_restrict[0] = True
            for c0, dst, nchunk in ((0, 0, 8), (2584, 8, 8), (4120, None, 4)):
                for blk in range(nchunk // 4):
                    W, wn = load_w(wblk(cd_w_in, c0 + blk * 512))
                    for j in range(4):
                        p, pn = psum()
                        for k in range(16):
                            mm(p[:, :NP], W[:, k, j * 128:(j + 1) * 128], HT[:, k, :NP], k == 0, k == 15, htr + [wn], [pn])
                        if dst is None:
                            cp(evac_eng(), GT[:, j, :NP], p[:, :NP], [pn], [("GT", j)])
                        else:
                            cp(evac_eng(), Q16[:, dst + blk * 4 + j, :NP], p[:, :NP], [pn], ["Q16"])
            Wg, wgn = load_w(wblk(cd_w_in, 2560, 24))
            Wi, win_ = load_w(wblk(cd_w_in, 4696, 8))
            for t in range(4):
                p, pn = psum()
                for k in range(16):
                    mm(p[:, 0:24], HT[:, k, t * 128:(t + 1) * 128], Wg[:, k, :], k == 0, k == 15, htr + [wgn], [pn])
                act(G[:, t, :], p[:, 0:24], AF.Sigmoid, [pn], ["G"])
                p, pn = psum()
                for k in range(16):
                    mm(p[:, 0:8], HT[:, k, t * 128:(t + 1) * 128], Wi[:, k, :], k == 0, k == 15, htr + [win_], [pn])
                ts("dve", WI[:, t, :], p[:, 0:8], 8.0 ** -0.5, None, ALU.mult, None, [pn], ["WI"])
            S.alias(["P", "PT", "OB", "JK"], htr)
            Lp = 1024 + 512 * (pp + 1)
            nsl, ntp = Lp // 64, Lp // 128
            for hf in range(2):
                dma("sp", XN[0][hf * 64:(hf + 1) * 64, 0:Lp], KI_s[:, 0:Lp], ["KIs"], [("XN", 0)])
            for g in range(2):
                KT, V, kvn = load_kv("cmp", g, 0, Lp)
                tt("dve", SC0[:, 0:Lp].rearrange("p (n j) -> p n j", j=64), KT[:, 0:Lp].rearrange("p (n j) -> p n j", j=64),
                   WCBv[:, 0, :, g].unsqueeze(1).to_broadcast([128, nsl, 64]), ALU.mult, [kvn, "WCB"], SC0n)
                S.op("dve", lambda e, nsl=nsl, Lp=Lp: e.tensor_reduce(out=SCC[0][:, 0, 0:nsl], in_=SC0[:, 0:Lp].rearrange("p (n j) -> p n j", j=64),
                                                                 axis=AX.X, op=ALU.add), SC0n, ["SCC0"])
                if nsl < 32:
                    memset("dve", CKT[:, g, nsl:32], 0.0, ["CKT"])
                cp("dve", CKT[:, g, 0:nsl], SCC[0][:, 0, 0:nsl], ["SCC0"], ["CKT"])
                pc, pcn = psum()
                for i in range(ntp):
                    mm(pc[0:32, 0:128], WBm[:, g, i, :], V[:, i, :], i == 0, i == ntp - 1, ["WBm", kvn], [pcn])
                cp("act", CV[:, g, :], pc[0:32, 0:128], [pcn], ["CV"])
            for qt in range(4 if CUT != 2 else 0):
                qt_g = 4 * pp + qt
                gq = 8 + qt_g
                qc = slice(qt * 128, (qt + 1) * 128)
                nk = gq + 1
                Lk = 128 * nk
                mi = nxt("m12", 2)
                dma("sp", M12[mi][:], selm_c[qt_g], [], [("M12", mi)])
                cp("dve", BC, B31[:, 0:8].unsqueeze(2).to_broadcast([128, 8, 32]), ["B31"], [BCn])
                ts("dve", BC[:, :, 0:16], BC[:, :, 0:16], FLAGB[:, 0:1], None, ALU.add, None, [BCn, "FLAGB"], [BCn])
                tt("dve", BC[:, :, 2 * gq:2 * gq + 2], BC[:, :, 2 * gq:2 * gq + 2], BCS[:, 0, :, :].rearrange("p j h -> p h j"), ALU.add, [BCn, "BCS"], [BCn])
                tt("dve", BC[:, :, 2 * gq - 2:2 * gq], BC[:, :, 2 * gq - 2:2 * gq], BCS[:, 1, :, :].rearrange("p j h -> p h j"), ALU.add, [BCn, "BCS"], [BCn])
                if 2 * gq + 2 < 32:
                    memset("dve", BC[:, :, 2 * gq + 2:32], NEG, [BCn])
                ts("dve", MK01, BC, -1e29, None, ALU.is_gt, None, [BCn], [BCn])
                for g in range(2):
                    pc, pcn = psum()
                    for r in range(4):
                        mm(pc[:, r * 32:(r + 1) * 32], Q16[:, g * 4 + r, qc], CKT[:, g, :], True, True, ["Q16", "CKT"], [pcn])
                    stt("dve", SCC[0][:], pc[:, 0:128].rearrange("p (r n) -> p r n", r=4), SCALE, BC[:, g * 4:(g + 1) * 4, :], ALU.mult, ALU.add,
                        [pcn, BCn], ["SCC0"])
                    st, stn = smcol(12)
                    S.op("dve", lambda e, st=st: e.tensor_reduce(out=st[:, 0:4], in_=SCC[0][:], axis=AX.X, op=ALU.max), ["SCC0"], stn)
                    tt("dve", SCC[0][:], SCC[0][:], st[:, 0:4].unsqueeze(2).to_broadcast([128, 4, 32]), ALU.subtract, ["SCC0"] + stn, ["SCC0"])
                    act(SCC[1][:], SCC[0][:], AF.Exp, ["SCC0"], ["SCC1"])
                    tt("dve", SCC[1][:], SCC[1][:], MK01[:, g * 4:(g + 1) * 4, :], ALU.mult, ["SCC1", BCn], ["SCC1"])
                    S.op("dve", lambda e, st=st: e.tensor_reduce(out=st[:, 4:8], in_=SCC[1][:], axis=AX.X, op=ALU.add), ["SCC1"], stn)
                    ts("dve", st[:, 4:8], st[:, 4:8], 1e-30, None, ALU.max, None, stn, stn)
                    recip(st[:, 8:12], st[:, 4:8], stn, stn)
                    tt("dve", SCC[1][:], SCC[1][:], st[:, 8:12].unsqueeze(2).to_broadcast([128, 4, 32]), ALU.mult, ["SCC1"] + stn, ["SCC1"])
                    cp("dve", PNB[:], SCC[1][:], ["SCC1"], ["PNB"])
                    S.op("dve", lambda e: e.tensor_reduce(out=IMP[:], in_=SCC[1][:].rearrange("p r n -> p n r"), axis=AX.X, op=ALU.add), ["SCC1"], ["IMP"])
                    tt("dve", IMP[:], IMP[:], M12[mi][:, 0, :], ALU.mult, ["IMP", ("M12", mi)], ["IMP"])
                    tt("dve", IMP[:], IMP[:], M12[mi][:, 1, :], ALU.add, ["IMP", ("M12", mi)], ["IMP"])
                    ts("dve", IMP[:, 0:16], IMP[:, 0:16], FM1[:, 0:1], None, ALU.add, None, ["IMP", "FM1"], ["IMP"])
                    S.op("dve", lambda e: e.max(out=T8[:, 0:8], in_=IMP[:]), ["IMP"], ["T8"])
                    S.op("dve", lambda e: e.match_replace(out=SCC[0][:, 0, :], in_to_replace=T8[:, 0:8], in_values=IMP[:], imm_value=-9.0),
                         ["IMP", "T8"], ["SCC0"])
                    S.op("dve", lambda e: e.max(out=T8[:, 8:16], in_=SCC[0][:, 0, :]), ["SCC0", "T8"], ["T8"])
                    ts("dve", MB[:, g, :], IMP[:], T8[:, 15:16], NEG, ALU.is_lt, ALU.mult, ["IMP", "T8"], ["MB"])
                    for r in range(4):
                        tr(TB[0][0:32, r * 128:(r + 1) * 128], PNB[:, r, :], IDB[:], ["PNB", "IDB"], [("TB", 0)])
                    cp("act", PCT[:], TB[0][0:32, 0:512].rearrange("p (r q) -> p r q", r=4), [("TB", 0)], ["PCT"])
                    po, pon = psum()
                    for r in range(4):
                        mm(po[:, r * 128:(r + 1) * 128], PCT[:, r, :], CV[:, g, :], True, True, ["PCT", "CV"], [pon])
                    for r in range(4):
                        h = g * 4 + r
                        oa, oan = oacc(h)
                        ts("dve", oa, po[:, r * 128:(r + 1) * 128], G[:, qt, h * 3:h * 3 + 1], None, ALU.mult, None, [pon, "G"], [oan])
                kv2 = [load_kv("sel", g, 0, Lk) for g in range(2)]
                for g in range(2 if CUT not in (3,) else 0):
                    KT, V, kvn = kv2[g]

                    def selmask(L, g=g):
                        tt("dve", SC0[:, 0:L].rearrange("p (n j) -> p n j", j=64), SC0[:, 0:L].rearrange("p (n j) -> p n j", j=64),
                           MB[:, g, 0:L // 64].unsqueeze(2).to_broadcast([128, L // 64, 64]), ALU.add, SC0n + ["MB"], SC0n)
                    for r in range(4):
                        h = g * 4 + r
                        dense_head(qc, qt_g, h, h, KT, V, kvn, 0, nk, selmask, G[:, qt, h * 3 + 1:h * 3 + 2], ["G"], False, False)
                kv2 = [load_kv("win", g, 128 * (gq - 4), 128 * (gq + 1)) for g in range(2)]
                for g in range(2 if CUT not in (3, 4) else 0):
                    KT, V, kvn = kv2[g]
                    for r in range(4):
                        h = g * 4 + r
                        dense_head(qc, qt_g, h, h, KT, V, kvn, gq - 4, 5, None, G[:, qt, h * 3 + 2:h * 3 + 3], ["G"], False, True)
                nch = (Lk + 511) // 512
                if CUT in (3, 4, 5):
                    continue
                for c in range(nch):
                    w_ = min(512, Lk - c * 512)
                    for hh in range(8):
                        pr = slice((hh % 2) * 64, (hh % 2) * 64 + 64)
                        p, pn = psum()
                        mm(p[:, 0:w_], GT[pr, hh // 2, qc], XN[0][pr, c * 512:c * 512 + w_], True, True, [("GT", hh // 2), ("XN", 0)], [pn])
                        rl, rln = wk()
                        act(rl[:, 0:w_], p[:, 0:w_], AF.Relu, [pn], [rln], scale=0.125)
                        if hh == 0:
                            ts("dve", SC1[:, c * 512:c * 512 + w_], rl[:, 0:w_], WI[:, qt, 0:1], None, ALU.mult, None, [rln, "WI"], SC1n)
                        else:
                            stt("dve", SC1[:, c * 512:c * 512 + w_], rl[:, 0:w_], WI[:, qt, hh:hh + 1], SC1[:, c * 512:c * 512 + w_], ALU.mult, ALU.add,
                                [rln, "WI"] + SC1n, SC1n)
                ts("dve", SC1[:, 0:1024], SC1[:, 0:1024], FLAGB[:, 0:1], None, ALU.add, None, SC1n + ["FLAGB"], SC1n)
                tt("dve", SC1[:, Lk - 128:Lk], SC1[:, Lk - 128:Lk], CAUS[:, :], ALU.add, SC1n + ["CAUS"], SC1n)
                bs, bsn = smcol(4)
                memset("dve", bs[:, 0:1], -64.0, bsn)
                for it in range(32):
                    step = 64.0 / (2 ** it)
                    ts("dve", bs[:, 1:2], bs[:, 0:1], step, None, ALU.add, None, bsn, bsn)
                    memset("dve", bs[:, 2:3], 0.0, bsn)
                    S.op("dve", lambda e, bs=bs, Lk=Lk: e.tensor_scalar(out=JK[:, 0:Lk], in0=SC1[:, 0:Lk], scalar1=bs[:, 1:2], scalar2=0.0,
                                                                        op0=ALU.is_ge, op1=ALU.add, accum_out=bs[:, 2:3]), SC1n + bsn, ["JK"] + bsn)
                    ts("dve", bs[:, 3:4], bs[:, 2:3], 256.0, step, ALU.is_ge, ALU.mult, bsn, bsn)
                    tt("dve", bs[:, 0:1], bs[:, 0:1], bs[:, 3:4], ALU.add, bsn, bsn)
                ts("dve", SC1[:, 0:Lk], SC1[:, 0:Lk], bs[:, 0:1], NEG, ALU.is_lt, ALU.mult, SC1n + bsn, SC1n)

                def dsamask(L):
                    tt("dve", SC0[:, 0:L], SC0[:, 0:L], SC1[:, 0:L], ALU.add, SC0n + SC1n, SC0n)
                kv2 = [load_kv("dsa", g, 0, Lk) for g in range(2)]
                for g in range(2):
                    KT, V, kvn = kv2[g]
                    for r in range(4):
                        h = 8 + g * 4 + r
                        dense_head(qc, qt_g, h, h, KT, V, kvn, 0, nk, dsamask, None, [], True, False)
                for q4 in range(4):
                    cp("act", OB[:, q4 * 512:(q4 + 1) * 512], OACC[q4][:, :], [OACCn[q4]], ["OB"])
                for hf in range(2):
                    for k in range(8):
                        kk = hf * 8 + k
                        tr(TB[hf][:, k * 128:(k + 1) * 128], OB[:, kk * 128:(kk + 1) * 128], IDB[:], ["OB", "IDB"], [("TB", hf)])
                    cp(evac_eng(), YCP[:, hf * 8:hf * 8 + 8, qc], TB[hf][:, :].rearrange("p (c n) -> p c n", c=8), [("TB", hf)],
                       [("YCP", c) for c in range(hf * 8, hf * 8 + 8)])
            S.alias(htr, ["P", "PT", "OB", "JK"])
            S.alias([("WP", 2)], ["Q16"])
            S.alias([("WP", 3)], [("KV", 0), ("KV", 1)])
            wp_restrict[0] = False

        def attn_sample():
            R = slice(0, 8)
            ht_all = [("HT", t) for t in range(4)]
            htr = [("HT", 0)]
            S.alias(["Q16", "KV1s"], [("WP", 2)])
            S.alias([("KV", 0), ("KV", 1)], [("WP", 3)])
            S.alias(["SC0s", "SC1s"], [("X", 1), ("X", 2), ("X", 3)])
            wp_restrict[0] = True
            Qs = WP[2][:, 0:2048].rearrange("p (h q) -> p h q", h=16)
            KTg = [WP[3][:, 0:2176], WP[2][:, 2048:4224]]
            Vg = [WP[3][:, 2176:4352].rearrange("p (t d) -> p t d", t=17), WP[2][:, 4224:6400].rearrange("p (t d) -> p t d", t=17)]
            kvg = [("KV", 0), "KV1s"]
            Xf = X[:, :, :].rearrange("p t d -> p (t d)")
            SC0s, SC1s = Xf[:, 2048:4224], Xf[:, 4352:6528]
            Pbs = HTf[:, 0:2176]
            PTbs = HTf[:, 2176:4352].rearrange("p (t q) -> p t q", t=17)
            OBs = HTf[:, 4352:6400]
            BUF_S = {"SC0": SC0s, "SC0n": ["SC0s"], "Pb": Pbs, "PTb": PTbs}
            PTI = UE[1][:, 0:256].bitcast(I32)
            IDXI = UE[1][:, 256:512].bitcast(I32)
            KIN = VE[0][:, 0:64].bitcast(BF16)
            GSA = VE[1][:, 0:384].rearrange("p (b c) -> p b c", b=16)
            WISA = VE[1][:, 384:512].rearrange("p (b c) -> p b c", b=16)
            PIDX = VE[1][:, 520:521]
            for c0, dst, nchunk in ((0, 0, 8), (2584, 8, 8), (4120, None, 4)):
                for blk in range(nchunk // 4):
                    W, wn = load_w(wblk(cd_w_in, c0 + blk * 512))
                    for j in range(4):
                        p, pn = psum()
                        for k in range(16):
                            mm(p[:, :128], W[:, k, j * 128:(j + 1) * 128], HT[:, k, :128], k == 0, k == 15, htr + [wn], [pn])
                        if dst is None:
                            cp("act", GT[:, j, :128], p[:, :128], [pn], [("GT", j)])
                        else:
                            cp("act", Qs[:, dst + blk * 4 + j, :], p[:, :128], [pn], ["Q16"])
            Wg, wgn = load_w(wblk(cd_w_in, 2560, 24))
            Wi, win_ = load_w(wblk(cd_w_in, 4696, 8))
            for b in range(16):
                qcb = slice(b * 8, (b + 1) * 8)
                pg, pgn_ = psum()
                for k in range(16):
                    mm(pg[R, 0:24], HT[:, k, qcb], Wg[:, k, :], k == 0, k == 15, htr + [wgn], [pgn_])
                act(GSA[R, b, :], pg[R, 0:24], AF.Sigmoid, [pgn_], [("VE", 1)])
                pw, pwn = psum()
                for k in range(16):
                    mm(pw[R, 0:8], HT[:, k, qcb], Wi[:, k, :], k == 0, k == 15, htr + [win_], [pwn])
                ts("dve", WISA[R, b, :], pw[R, 0:8], 8.0 ** -0.5, None, ALU.mult, None, [pwn], [("VE", 1)])
            S.alias(["P", "PT", "OB", "JK"], ht_all)
            dma("sp", PTI, pt16.partition_broadcast(128), [], [("UE", 1)])
            dma("sp", PIDX, pidx_c[:, :], [], [("VE", 1)])
            ts("dve", IDXI, PTI, 128.0, PIDX, ALU.mult, ALU.add, [("UE", 1), ("VE", 1)], [("UE", 1)])
            memset("pool", KIN[:, :], 0.0, [("VE", 0)])

            def page_to_kv(pgf, pgn, j, ntl_):
                di = nxt("vst", 2)
                cp("act", DT[:, di, :], pgf[:, 0:512], [pgn], [("DT", di)])
                for g in range(2):
                    tr(TB[g][:, (j % 8) * 128:(j % 8 + 1) * 128], DT[:, di, g * 128:(g + 1) * 128], IDB[:], [("DT", di), "IDB"], [("TB", g)])
                    if j % 8 == 7 or j == ntl_ - 1:
                        n8 = j % 8 + 1
                        j0 = j - (n8 - 1)
                        cp("act", KTg[g][:, j0 * 128:(j + 1) * 128], TB[g][:, 0:n8 * 128], [("TB", g)], [kvg[g]])
                    cp("act", Vg[g][:, j, :], DT[:, di, 256 + g * 128:256 + (g + 1) * 128], [("DT", di)], [kvg[g]])

            def new_tile(nm, b, ti):
                for g in range(2):
                    memset("pool", KTg[g][:, ti * 128:(ti + 1) * 128], 0.0, [kvg[g]])
                    memset("pool", Vg[g][:, ti, :], 0.0, [kvg[g]])
                    dma("sp", KTg[g][:, ti * 128:ti * 128 + 8], KT_s[nm][g, :, b * 8:(b + 1) * 8], [("KTs", nm)], [kvg[g]])
                    dma("sp", Vg[g][0:8, ti, :], V_s[nm][b * 8:(b + 1) * 8, g * 128:(g + 1) * 128], [("Vs", nm)], [kvg[g]])

            def build_kv(nm, b):
                for j in range(16):
                    pgf, pgn = wk()
                    col = b * 16 + j
                    S.op("pool", lambda e, pgf=pgf, col=col, nm=nm: e.indirect_dma_start(
                        out=pgf[:, 0:512], out_offset=None, in_=cache2d[nm][:, :],
                        in_offset=bass.IndirectOffsetOnAxis(ap=IDXI[:, col:col + 1], axis=0)), [("UE", 1)], [pgn], dma=True)
                    page_to_kv(pgf, pgn, j, 16)
                new_tile(nm, b, 16)

            for b in range(16):
                qcb = slice(b * 8, (b + 1) * 8)
                GS = GSA[:, b, :]
                WIS = WISA[:, b, :]
                gsn = [("VE", 1)]
                build_kv("cmp", b)
                cp("act", BC[R], B31[R, 0:8].unsqueeze(2).to_broadcast([8, 8, 32]), ["B31"], [BCn])
                tt("dve", BC[R, :, 30:32], BC[R, :, 30:32], BCS[R, 1, :, :].rearrange("p j h -> p h j"), ALU.add, [BCn, "BCS"], [BCn])
                for g in range(2):
                    tt("dve", SC0s[:, 0:2048].rearrange("p (n j) -> p n j", j=64), KTg[g][:, 0:2048].rearrange("p (n j) -> p n j", j=64),
                       WCBv[:, 0, :, g].unsqueeze(1).to_broadcast([128, 32, 64]), ALU.mult, [kvg[g], "WCB"], ["SC0s"])
                    S.op("dve", lambda e: e.tensor_reduce(out=SCC[0][:, 0, :], in_=SC0s[:, 0:2048].rearrange("p (n j) -> p n j", j=64),
                                                          axis=AX.X, op=ALU.add), ["SC0s"], ["SCC0"])
                    cp("act", CKT[:, g, :], SCC[0][:, 0, :], ["SCC0"], ["CKT"])
                    pc, pcn = psum()
                    for i in range(16):
                        mm(pc[0:32, 0:128], WBm[:, g, i, :], Vg[g][:, i, :], i == 0, i == 15, ["WBm", kvg[g]], [pcn])
                    cp("act", CV[:, g, :], pc[0:32, 0:128], [pcn], ["CV"])
                    pc, pcn = psum()
                    for r in range(4):
                        mm(pc[R, r * 32:(r + 1) * 32], Qs[:, g * 4 + r, qcb], CKT[:, g, :], True, True, ["Q16", "CKT"], [pcn])
                    stt("dve", SCC[0][R], pc[R, 0:128].rearrange("p (r n) -> p r n", r=4), SCALE, BC[R, g * 4:(g + 1) * 4, :], ALU.mult, ALU.add,
                        [pcn, BCn], ["SCC0"])
                    st, stn = smcol(12)
                    S.op("dve", lambda e, st=st: e.tensor_reduce(out=st[R, 0:4], in_=SCC[0][R], axis=AX.X, op=ALU.max), ["SCC0"], stn)
                    tt("dve", SCC[0][R], SCC[0][R], st[R, 0:4].unsqueeze(2).to_broadcast([8, 4, 32]), ALU.subtract, ["SCC0"] + stn, ["SCC0"])
                    act(SCC[1][R], SCC[0][R], AF.Exp, ["SCC0"], ["SCC1"])
                    S.op("dve", lambda e, st=st: e.tensor_reduce(out=st[R, 4:8], in_=SCC[1][R], axis=AX.X, op=ALU.add), ["SCC1"], stn)
                    recip(st[R, 8:12], st[R, 4:8], stn, stn)
                    tt("dve", SCC[1][R], SCC[1][R], st[R, 8:12].unsqueeze(2).to_broadcast([8, 4, 32]), ALU.mult, ["SCC1"] + stn, ["SCC1"])
                    cp("act", PNB[R], SCC[1][R], ["SCC1"], ["PNB"])
                    S.op("dve", lambda e: e.tensor_reduce(out=IMP[R, :], in_=SCC[1][R].rearrange("p r n -> p n r"), axis=AX.X, op=ALU.add), ["SCC1"], ["IMP"])
                    S.op("dve", lambda e: e.max(out=T8[R, 0:8], in_=IMP[R, :]), ["IMP"], ["T8"])
                    S.op("dve", lambda e: e.match_replace(out=SCC[0][R, 0, :], in_to_replace=T8[R, 0:8], in_values=IMP[R, :], imm_value=-9.0),
                         ["IMP", "T8"], ["SCC0"])
                    S.op("dve", lambda e: e.max(out=T8[R, 8:16], in_=SCC[0][R, 0, :]), ["SCC0", "T8"], ["T8"])
                    ts("dve", MB[R, g, :], IMP[R, :], T8[R, 14:15], NEG, ALU.is_lt, ALU.mult, ["IMP", "T8"], ["MB"])
                    for r in range(4):
                        tr(TB[0][0:32, r * 8:(r + 1) * 8], PNB[R, r, :], IDB[R, R], ["PNB", "IDB"], [("TB", 0)])
                    cp("act", PCT[:, :, 0:8], TB[0][0:32, 0:32].rearrange("p (r q) -> p r q", r=4), [("TB", 0)], ["PCT"])
                    po, pon = psum()
                    for r in range(4):
                        mm(po[R, r * 128:(r + 1) * 128], PCT[:, r, 0:8], CV[:, g, :], True, True, ["PCT", "CV"], [pon])
                    for r in range(4):
                        h = g * 4 + r
                        oa, oan = oacc(h)
                        ts("dve", oa[R, :], po[R, r * 128:(r + 1) * 128], GS[R, h * 3:h * 3 + 1], None, ALU.mult, None, [pon] + gsn, [oan])
                build_kv("sel", b)
                for g in range(2):
                    def selmask_s(L, g=g):
                        tt("dve", SC0s[R, 0:2048].rearrange("p (n j) -> p n j", j=64), SC0s[R, 0:2048].rearrange("p (n j) -> p n j", j=64),
                           MB[R, g, :].unsqueeze(2).to_broadcast([8, 32, 64]), ALU.add, ["SC0s", "MB"], ["SC0s"])
                    for r in range(4):
                        h = g * 4 + r
                        dense_head(None, 8, h, h, KTg[g], Vg[g], kvg[g], 0, 17, selmask_s, GS[R, h * 3 + 1:h * 3 + 2], gsn, False, False,
                                   nq=8, B=BUF_S, noflag=True, qap=Qs[:, h, qcb])
                for j in range(4):
                    pgf, pgn = wk()
                    dma("sp", pgf[:, 0:512], winb[b, j * 128:(j + 1) * 128, :], [], [pgn])
                    page_to_kv(pgf, pgn, j, 4)
                new_tile("win", b, 4)
                for g in range(2):
                    for r in range(4):
                        h = g * 4 + r
                        dense_head(None, 8, h, h, KTg[g][:, 0:640], Vg[g], kvg[g], 12, 5, None, GS[R, h * 3 + 2:h * 3 + 3], gsn, False, True,
                                   nq=8, B=BUF_S, noflag=True, qap=Qs[:, h, qcb])
                for j in range(16):
                    pgf, pgn = wk()
                    col = b * 16 + j
                    S.op("pool", lambda e, pgf=pgf, col=col: e.indirect_dma_start(
                        out=pgf[:, 0:64], out_offset=None, in_=cache_idx2d[:, :],
                        in_offset=bass.IndirectOffsetOnAxis(ap=IDXI[:, col:col + 1], axis=0)), [("UE", 1)], [pgn], dma=True)
                    di = nxt("vst", 2)
                    cp("act", DT[:, di, 0:64], pgf[:, 0:64], [pgn], [("DT", di)])
                    cp("act", DT[:, di, 64:128], pgf[:, 0:64], [pgn], [("DT", di)])
                    hf = (j // 8) % 2
                    tr(TB[hf][:, (j % 8) * 128:(j % 8 + 1) * 128], DT[:, di, 0:128], IDB[:], [("DT", di), "IDB"], [("TB", hf)])
                    if j % 8 == 7:
                        cp("act", XN[0][:, (j - 7) * 128:(j + 1) * 128], TB[hf][:, 0:1024], [("TB", hf)], [("XN", 0)])
                for hf in range(2):
                    dma("sp", KIN[hf * 64:(hf + 1) * 64, 0:8], KI_s[:, b * 8:(b + 1) * 8], ["KIs"], [("VE", 0)])
                for c in range(5):
                    w_ = 512 if c < 4 else 128
                    for hh in range(8):
                        pr = slice((hh % 2) * 64, (hh % 2) * 64 + 64)
                        p, pn = psum()
                        if c < 4:
                            mm(p[R, 0:w_], GT[pr, hh // 2, qcb], XN[0][pr, c * 512:(c + 1) * 512], True, True, [("GT", hh // 2), ("XN", 0)], [pn])
                        else:
                            mm(p[R, 0:w_], GT[pr, hh // 2, qcb], KIN[pr, :], True, True, [("GT", hh // 2), ("VE", 0)], [pn])
                        rl, rln = wk()
                        act(rl[R, 0:w_], p[R, 0:w_], AF.Relu, [pn], [rln], scale=0.125)
                        if hh == 0:
                            ts("dve", SC1s[R, c * 512:c * 512 + w_], rl[R, 0:w_], WIS[R, 0:1], None, ALU.mult, None, [rln] + gsn, ["SC1s"])
                        else:
                            stt("dve", SC1s[R, c * 512:c * 512 + w_], rl[R, 0:w_], WIS[R, hh:hh + 1], SC1s[R, c * 512:c * 512 + w_], ALU.mult, ALU.add,
                                [rln, "SC1s"] + gsn, ["SC1s"])
                tt("dve", SC1s[R, 2048:2176], SC1s[R, 2048:2176], CAUS[R, :], ALU.add, ["SC1s", "CAUS"], ["SC1s"])
                bs, bsn = smcol(4)
                memset("dve", bs[R, 0:1], -64.0, bsn)
                for it in range(32):
                    step = 64.0 / (2 ** it)
                    ts("dve", bs[R, 1:2], bs[R, 0:1], step, None, ALU.add, None, bsn, bsn)
                    memset("dve", bs[R, 2:3], 0.0, bsn)
                    S.op("dve", lambda e, bs=bs: e.tensor_scalar(out=SC0s[R, 0:2176], in0=SC1s[R, 0:2176], scalar1=bs[R, 1:2], scalar2=0.0,
                                                                 op0=ALU.is_ge, op1=ALU.add, accum_out=bs[R, 2:3]), ["SC1s"] + bsn, ["SC0s"] + bsn)
                    ts("dve", bs[R, 3:4], bs[R, 2:3], 256.0, step, ALU.is_ge, ALU.mult, bsn, bsn)
                    tt("dve", bs[R, 0:1], bs[R, 0:1], bs[R, 3:4], ALU.add, bsn, bsn)
                ts("dve", SC1s[R, 0:2176], SC1s[R, 0:2176], bs[R, 0:1], NEG, ALU.is_lt, ALU.mult, ["SC1s"] + bsn, ["SC1s"])

                def dsamask_s(L):
                    tt("dve", SC0s[R, 0:L], SC0s[R, 0:L], SC1s[R, 0:L], ALU.add, ["SC0s", "SC1s"], ["SC0s"])
                build_kv("dsa", b)
                for g in range(2):
                    for r in range(4):
                        h = 8 + g * 4 + r
                        dense_head(None, 8, h, h, KTg[g], Vg[g], kvg[g], 0, 17, dsamask_s, None, [], True, False,
                                   nq=8, B=BUF_S, noflag=True, qap=Qs[:, h, qcb])
                for q4 in range(4):
                    cp("act", OBs[R, q4 * 512:(q4 + 1) * 512], OACC[q4][R, :], [OACCn[q4]], ["OB"])
                for hf in range(2):
                    for k in range(8):
                        kk = hf * 8 + k
                        tr(TB[hf][:, k * 8:(k + 1) * 8], OBs[R, kk * 128:(kk + 1) * 128], IDB[R, R], ["OB", "IDB"], [("TB", hf)])
                    cp("act", YCP[:, hf * 8:hf * 8 + 8, qcb], TB[hf][:, 0:64].rearrange("p (c n) -> p c n", c=8), [("TB", hf)],
                       [("YCP", c) for c in range(hf * 8, hf * 8 + 8)])
            S.alias(ht_all, ["P", "PT", "OB", "JK"])
            S.alias([("WP", 2)], ["Q16", "KV1s"])
            S.alias([("WP", 3)], [("KV", 0), ("KV", 1)])
            S.alias([("X", 1), ("X", 2), ("X", 3)], ["SC0s", "SC1s"])
            wp_restrict[0] = False

        def final_out(dst, ntl):
            gf = CTf[:, 0:2048]
            yt = CTf[:, 2048:4096]
            dma("sp", gf, norm_final.partition_broadcast(128), [], SC0n)
            for t in range(ntl):
                c = nxt("stat", 16)
                act(XN[0][:], X[:, t, :], AF.Square, [("X", t)], [("XN", 0), ("SS", c)], accum_out=SS[:, c:c + 1])
                act(RS[:, c:c + 1], SS[:, c:c + 1], AF.Sqrt, [("SS", c), "EPSC"], [("RS", c)], scale=1.0 / D, bias=EPSC[:, 0:1])
                recip(RS[:, c:c + 1], RS[:, c:c + 1], [("RS", c)], [("RS", c)])
                act(yt, X[:, t, :], AF.Identity, [("X", t), ("RS", c)], SC1n, scale=RS[:, c:c + 1])
                tt("dve", yt, yt, gf, ALU.mult, SC0n + SC1n, SC1n)
                dma("sp", dst[t * 128:(t + 1) * 128, :], yt, SC1n, [])

        for b in range(16):
            dma("sp", win_s[b, 0:504, :], winb[b, 8:512, :], [], [])

        if CUT not in (6, 7, 8):
            attn_setup()
        passes = [("ctx", 0, xc), ("ctx", 1, xc), ("own", 0, xo), ("own", 1, xo), ("smp", 0, xs)]
        if CUT == 8:
            passes = [("ctx", 0, xc), ("own", 0, xo)]
        for kind, pp, xsrc in passes:
            NP = 128 if kind == "smp" else 512
            ntl = NP // 128
            src = xsrc if kind == "smp" else xsrc[pp * 512:(pp + 1) * 512, :]
            load_x(src, ntl)
            norm_to_ht(0, ntl)
            l0_mixer(kind, pp, NP, ntl)
            proj_residual(ab_w_out, ntl, YCP, "YCP", 16)
            norm_to_ht(1, ntl)
            ffn(0, NP, ntl)
            norm_to_ht(2, ntl)
            l1_kv_proj(kind, pp, NP, ntl)
            if kind == "own" and CUT not in (6, 7, 8):
                if CUT != 1:
                    attn_prompt(pp, None)
                proj_residual(cd_w_out, ntl, YCP, "YCP", 16)
                norm_to_ht(3, ntl)
                ffn(1, NP, ntl)
                final_out(y_o[pp * 512:(pp + 1) * 512, :], ntl)
            if kind == "smp" and SAMPLE_ATTN and CUT not in (1, 6, 7, 8, 9):
                attn_sample()
                proj_residual(cd_w_out, ntl, YCP, "YCP", 16)
                norm_to_ht(3, ntl)
                ffn(1, NP, ntl)
                final_out(y_s, ntl)

        S.final_wait_all("sp")
        if os.environ.get("KDEBUG"):
            print("op counts", S.cnt, "dma", S.dma_rr, max(S.dma_cnt.values()))
            print("simulate ok:", S.simulate())
        S.emit()
    return nc


_PROG = {}


def _invcnt_table(h):
    t = np.zeros((5, 4, 512), np.float32)
    starts = [0, 512, 1024 * h, 1024 * h + 512, 2048]
    for p, s in enumerate(starts):
        pos = s + np.arange(512)
        if p == 4:
            pos = 2048 + (np.arange(512) % 8)
        for g, w in enumerate(POOL_WINDOWS):
            t[p, g] = 1.0 / np.minimum(pos + 1, w)
    return t


def kernel(**inp):
    f = lambda k: np.ascontiguousarray(np.asarray(inp[k]))
    x_prompt, x_sample = f("x_prompt"), f("x_sample")
    state_conv, state_pool = f("state_conv"), f("state_pool")
    cache_win = f("cache_nsa_win")
    if "nc" not in _PROG:
        _PROG["nc"] = build_program()
    nc = _PROG["nc"]
    shared = {
        "ident": np.eye(128, dtype=np.float32),
        "norm_mix": f("norm_mix"), "norm_ffn": f("norm_ffn"), "norm_final": f("norm_final"),
        "ab_w_in": f("ab_w_in")[0], "ab_conv_w": f("ab_conv_w")[0], "ab_conv_b": f("ab_conv_b")[0],
        "ab_ln_g": f("ab_ln_g")[0], "ab_ln_b": f("ab_ln_b")[0], "ab_pool_w": f("ab_pool_w")[0],
        "ab_pool_scale": f("ab_pool_scale")[0], "ab_w_out": f("ab_w_out")[0], "cd_w_in": f("cd_w_in")[0],
        "ffn_w1": f("ffn_w1"), "ffn_w2": f("ffn_w2"),
        "cd_w_cmp": np.ascontiguousarray(f("cd_w_cmp")[0].reshape(256)),
        "cd_w_out": f("cd_w_out")[0],
        "rel_bias": np.ascontiguousarray(f("rel_bias").reshape(512)),
    }
    qi = np.arange(128)[:, None]
    kj = np.arange(128)[None, :]
    shared["caus_c"] = np.where(kj <= qi, 0.0, -1e30).astype(np.float32)
    shared["wm4_c"] = np.where(kj > qi, 0.0, -1e30).astype(np.float32)
    shared["dist0_c"] = (qi - kj).astype(np.float32)
    mb = np.zeros((128, 16, 32), np.float32)
    for i in range(16):
        mb[:64, i, 2 * i] = 1.0
        mb[64:, i, 2 * i + 1] = 1.0
    shared["mskblk_c"] = mb
    selm = np.zeros((8, 128, 2, 32), np.float32)
    for qg in range(8):
        cur = 16 + (qg * 128 + np.arange(128)) // 64
        n = np.arange(32)[None, :]
        curm = (n == cur[:, None]).astype(np.float32)
        futm = (n > cur[:, None]).astype(np.float32)
        selm[qg, :, 0, :] = 1.0 - curm - futm
        selm[qg, :, 1, :] = 2.0 * curm - futm
    shared["selm_c"] = selm
    page_table = f("page_table")
    if SAMPLE_ATTN:
        shared["pidx_c"] = np.arange(128, dtype=np.float32).reshape(128, 1)
        shared["cache_cmp"] = f("cache_nsa_cmp")[0].reshape(2560 * 128, 512)
        shared["cache_sel"] = f("cache_nsa_sel")[0].reshape(2560 * 128, 512)
        shared["cache_dsa"] = f("cache_dsa_kv")[0].reshape(2560 * 128, 512)
        shared["cache_idx"] = f("cache_dsa_idx")[0].reshape(2560 * 128, 64)
    in_maps = []
    for c in range(NCORES):
        s, h = c // 2, c % 2
        m = dict(shared)
        m["xc"] = np.ascontiguousarray(x_prompt[s, 0:1024])
        m["xo"] = np.ascontiguousarray(x_prompt[s, 1024 * h:1024 * h + 1024])
        m["xs"] = np.ascontiguousarray(x_sample[16 * c:16 * c + 16].reshape(128, D))
        m["sconv"] = np.ascontiguousarray(state_conv[0, 16 * c:16 * c + 16])
        m["spool"] = np.ascontiguousarray(state_pool[0, 16 * c:16 * c + 16])
        m["winb"] = np.ascontiguousarray(cache_win[0, 16 * c:16 * c + 16].reshape(16, 512, 512))
        m["flag"] = np.full((128, 1), float(h), np.float32)
        m["flagb"] = np.full((128, 1), 0.0 if h == 1 else -1e30, np.float32)
        m["fm1"] = np.full((128, 1), float(h) - 1.0, np.float32)
        if SAMPLE_ATTN:
            m["pt16"] = np.ascontiguousarray(page_table[16 * c:16 * c + 16].reshape(256).astype(np.int32))
        m["invcnt"] = _invcnt_table(h)
        in_maps.append(m)
    res = run_bass_kernel_spmd(nc, in_maps, core_ids=list(range(NCORES)))
    R = res.results
    B, T = 4, 2048
    y_prompt = np.zeros((B, T, D), np.float32)
    y_sample = np.zeros((128, 8, D), np.float32)
    conv_p = np.zeros((1, B, 30, DC), np.float32)
    conv_s = np.zeros((1, 128, 30, DC), np.float32)
    pool_p = np.zeros((1, B, 15, DC), np.float32)
    pool_s = np.zeros((1, 128, 15, DC), np.float32)
    kvp = {k: np.zeros((1, B, T, 2, 2, 128), np.float32) for k in ("cmp", "sel", "dsa")}
    kvs = {k: np.zeros((1, 128, 8, 2, 2, 128), np.float32) for k in ("cmp", "sel", "dsa")}
    win_p = np.zeros((1, B, 512, 2, 2, 128), np.float32)
    win_s = np.zeros((1, 128, 512, 2, 2, 128), np.float32)
    idx_p = np.zeros((1, B, T, 64), np.float32)
    idx_s = np.zeros((1, 128, 8, 64), np.float32)
    for c in range(NCORES):
        s, h = c // 2, c % 2
        r = R[c]
        sl = slice(1024 * h, 1024 * h + 1024)
        bs = slice(16 * c, 16 * c + 16)
        y_prompt[s, sl] = r["y_o"]
        y_sample[bs] = r["y_s"].reshape(16, 8, D)
        conv_s[0, bs] = r["conv_s"]
        pool_s[0, bs] = r["pool_s"]
        for k in ("cmp", "sel", "dsa"):
            kvp[k][0, s, sl] = r[k + "_o"].reshape(1024, 2, 2, 128)
            kvs[k][0, bs] = r[k + "_s"].reshape(16, 8, 2, 2, 128)
        idx_p[0, s, sl] = r["idx_o"]
        idx_s[0, bs] = r["idx_s"].reshape(16, 8, 64)
        win_s[0, bs] = r["win_s"].reshape(16, 512, 2, 2, 128)
        if h == 1:
            conv_p[0, s] = r["conv_o"]
            pool_p[0, s] = r["pool_o"]
            win_p[0, s] = r["win_o"][512:1024].reshape(512, 2, 2, 128)
    return (y_prompt, y_sample, conv_p, conv_s, pool_p, pool_s, kvp["cmp"], kvs["cmp"], kvp["sel"], kvs["sel"],
            win_p, win_s, kvp["dsa"], kvs["dsa"], idx_p, idx_s)
```
